# Optimizing a Trainium2 kernel written in Bass

```python
import jax, jax.numpy as jnp
from jax import lax
import numpy as np

D_MODEL = 1024
BATCH = 8
SEQ = 4096
DEPTH = 1

CHUNK = 64
RET_HEADS = 4
RET_QK_DIM = 256
RET_V_DIM = 256
RET_QK = RET_HEADS * RET_QK_DIM
RET_V = RET_HEADS * RET_V_DIM
POOL_WINDOWS = (2, 4, 8, 16)
POOL_GROUPS = 4
POOL_GROUP_DIM = 256
POOL_WIDTH = POOL_GROUPS * POOL_GROUP_DIM
N_BRANCH = 2
IN_WIDTH = 2 * RET_QK + 2 * RET_V + POOL_WIDTH + N_BRANCH * D_MODEL
D_FF = 2816
ROPE_BASE = 10000.0
NORM_EPS = 1e-6
FFN_RES_WEIGHT = 0.5

kernel_name = "hybrid_retention_pool_macaron"


def rmsnorm(x, g):
    xf = x.astype(jnp.float32)
    y = xf * lax.rsqrt(jnp.mean(xf * xf, axis=-1, keepdims=True) + NORM_EPS)
    return (y * g.astype(jnp.float32)).astype(x.dtype)


def swiglu_ffn(x, w_in, w_out):
    gate, up = jnp.split(x @ w_in, 2, axis=-1)
    return (jax.nn.silu(gate) * up) @ w_out


def rotary(x):
    s, d = x.shape[1], x.shape[-1]
    half = d // 2
    inv_freq = ROPE_BASE ** (-jnp.arange(half, dtype=jnp.float32) / half)
    ang = jnp.arange(s, dtype=jnp.float32)[:, None] * inv_freq[None, :]
    cos = jnp.cos(ang)[None, :, None, :].astype(x.dtype)
    sin = jnp.sin(ang)[None, :, None, :].astype(x.dtype)
    x1, x2 = x[..., :half], x[..., half:]
    return jnp.concatenate([x1 * cos - x2 * sin, x1 * sin + x2 * cos], axis=-1)


def retention(q, k, v):
    b, s, h, dk = q.shape
    dv = v.shape[-1]
    nc = s // CHUNK
    log_gamma = jnp.log(1.0 - 2.0 ** (-5.0 - jnp.arange(h, dtype=jnp.float32)))
    idx = jnp.arange(CHUNK, dtype=jnp.float32)
    inner_decay = jnp.exp(log_gamma[:, None, None] * jnp.abs(idx[:, None] - idx[None, :]))
    q_decay = jnp.exp(log_gamma[None, :] * (idx[:, None] + 1.0))
    k_decay = jnp.exp(log_gamma[None, :] * (CHUNK - 1.0 - idx[:, None]))
    chunk_decay = jnp.exp(log_gamma * CHUNK)

    qc = q.reshape(b, nc, CHUNK, h, dk)
    kc = k.reshape(b, nc, CHUNK, h, dk)
    vc = v.reshape(b, nc, CHUNK, h, dv)

    scores = jnp.einsum("bnchd,bnshd->bnhcs", qc, kc) * inner_decay[None, None]
    inner = jnp.einsum("bnhcs,bnshe->bnche", scores, vc)

    def step(state, inp):
        q_i, k_i, v_i = inp
        cross = jnp.einsum("bchd,bhde->bche", q_i * q_decay[None, :, :, None], state)
        new_state = state * chunk_decay[None, :, None, None] + jnp.einsum(
            "bchd,bche->bhde", k_i * k_decay[None, :, :, None], v_i)
        return new_state, cross

    state0 = jnp.zeros((b, h, dk, dv), jnp.float32)
    xs = (qc.transpose(1, 0, 2, 3, 4), kc.transpose(1, 0, 2, 3, 4), vc.transpose(1, 0, 2, 3, 4))
    _, cross = lax.scan(step, state0, xs)
    out = inner + cross.transpose(1, 0, 2, 3, 4)
    return out.reshape(b, s, h, dv)


def multiscale_pool(p, w_group, scale):
    b, s, _ = p.shape
    pf = p.astype(jnp.float32).reshape(b, s, POOL_GROUPS, POOL_GROUP_DIM)
    cs = jnp.concatenate([jnp.zeros((b, 1, POOL_GROUPS, POOL_GROUP_DIM), jnp.float32),
                          jnp.cumsum(pf, axis=1)], axis=1)
    t = jnp.arange(s, dtype=jnp.float32)
    outs = []
    for g, w in enumerate(POOL_WINDOWS):
        cs_g = cs[:, :, g]
        shifted = jnp.pad(cs_g[:, : s + 1 - w], ((0, 0), (w - 1, 0), (0, 0)))
        count = jnp.minimum(t + 1.0, float(w))[None, :, None]
        outs.append((cs_g[:, 1:] - shifted) / count - pf[:, :, g])
    pooled = jnp.stack(outs, axis=2)
    mixed = jnp.einsum("bsgc,gcd->bsgd", pooled, w_group.astype(jnp.float32))
    return (mixed.reshape(b, s, POOL_WIDTH) * scale.astype(jnp.float32)).astype(p.dtype)


def setup_inputs(seed: int = 0) -> dict:
    key = jax.random.key(seed)
    ks = jax.random.split(key, 20)
    f32 = jnp.float32

    def nrm(k, shape, fan_in):
        return jax.random.normal(k, shape, f32) * (fan_in ** -0.5)

    def gain(k, shape):
        return 1.0 + 0.02 * jax.random.normal(k, shape, f32)

    return {
        "x": jax.random.normal(ks[0], (BATCH, SEQ, D_MODEL), f32),
        "norm_ffn1": gain(ks[1], (DEPTH, D_MODEL)),
        "ffn1_w_in": nrm(ks[2], (DEPTH, D_MODEL, 2 * D_FF), D_MODEL),
        "ffn1_w_out": nrm(ks[3], (DEPTH, D_FF, D_MODEL), D_FF),
        "norm_mix": gain(ks[4], (DEPTH, D_MODEL)),
        "w_in": nrm(ks[5], (DEPTH, D_MODEL, IN_WIDTH), D_MODEL),
        "gate_bias": 0.02 * jax.random.normal(ks[6], (DEPTH, N_BRANCH, D_MODEL), f32),
        "pool_w": nrm(ks[7], (DEPTH, POOL_GROUPS, POOL_GROUP_DIM, POOL_GROUP_DIM), POOL_GROUP_DIM),
        "pool_scale": gain(ks[8], (DEPTH, POOL_WIDTH)),
        "w_ret_up": nrm(ks[9], (DEPTH, RET_V, D_MODEL), RET_V),
        "w_pool_up": nrm(ks[10], (DEPTH, POOL_WIDTH, D_MODEL), POOL_WIDTH),
        "w_out": nrm(ks[11], (DEPTH, D_MODEL, D_MODEL), D_MODEL),
        "norm_ffn2": gain(ks[12], (DEPTH, D_MODEL)),
        "ffn2_w_in": nrm(ks[13], (DEPTH, D_MODEL, 2 * D_FF), D_MODEL),
        "ffn2_w_out": nrm(ks[14], (DEPTH, D_FF, D_MODEL), D_FF),
        "norm_final": gain(ks[15], (D_MODEL,)),
    }


def reference(x, norm_ffn1, ffn1_w_in, ffn1_w_out, norm_mix, w_in, gate_bias, pool_w,
              pool_scale, w_ret_up, w_pool_up, w_out, norm_ffn2, ffn2_w_in, ffn2_w_out,
              norm_final):
    b, s, _ = x.shape
    split_points = [RET_QK, 2 * RET_QK, 2 * RET_QK + RET_V, 2 * RET_QK + 2 * RET_V,
                    2 * RET_QK + 2 * RET_V + POOL_WIDTH]
    h = x
    for l in range(DEPTH):
        h = h + FFN_RES_WEIGHT * swiglu_ffn(rmsnorm(h, norm_ffn1[l]), ffn1_w_in[l], ffn1_w_out[l])

        u = rmsnorm(h, norm_mix[l])
        proj = u @ w_in[l]
        q, k, v, g_ret, p, gates = jnp.split(proj, split_points, axis=-1)
        q = rotary(q.reshape(b, s, RET_HEADS, RET_QK_DIM))
        k = rotary(k.reshape(b, s, RET_HEADS, RET_QK_DIM)) * (RET_QK_DIM ** -0.5)
        v = v.reshape(b, s, RET_HEADS, RET_V_DIM)
        ret = retention(q.astype(jnp.float32), k.astype(jnp.float32), v.astype(jnp.float32))
        ret = ret * lax.rsqrt(jnp.mean(ret * ret, axis=-1, keepdims=True) + NORM_EPS)
        ret = ret.reshape(b, s, RET_V).astype(proj.dtype) * jax.nn.silu(g_ret)

        pool_out = multiscale_pool(p, pool_w[l], pool_scale[l])

        gate = jax.nn.sigmoid(gates.reshape(b, s, N_BRANCH, D_MODEL) + gate_bias[l])
        merged = gate[:, :, 0] * (ret @ w_ret_up[l]) + gate[:, :, 1] * (pool_out @ w_pool_up[l])
        h = h + merged @ w_out[l]

        h = h + FFN_RES_WEIGHT * swiglu_ffn(rmsnorm(h, norm_ffn2[l]), ffn2_w_in[l], ffn2_w_out[l])
    return rmsnorm(h, norm_final)
```

```python
import math
import os
import numpy as np
import concourse.bass as bass
import concourse.mybir as mybir
from concourse.bass_utils import run_bass_kernel_spmd

F32 = mybir.dt.float32
BF16 = mybir.dt.bfloat16
AF = mybir.ActivationFunctionType
ALU = mybir.AluOpType

D = 1024
S = 4096 if not os.environ.get('KDBG_NT') else 512 * int(os.environ['KDBG_NT'])
DFF = 2816
T = 512
NT = S // T
KC = 8
HC = DFF // 128
NSLOT = 4
SLOT = 4096
EPS = 1e-6
LG = [math.log(1.0 - 2.0 ** (-5.0 - h)) for h in range(4)]
LN16 = math.log(1.0 / 16.0)
PI = math.pi
HALO = 16
PL = HALO + T

C_INVF = 0
C_TLOC = 1
C_ID = C_TLOC + T
NCONST = C_ID + 128
S_DIST = 0
S_VALID = S_DIST + 4 * 128
S_KREV = S_VALID + 128
S_CNT = S_KREV + 4
NSET = S_CNT + 64
V_G1, V_GM, V_G2, V_GF, V_GB0, V_GB1, V_PS = 0, 8, 16, 24, 32, 40, 48
NVEC = 56


class Buf:
    __slots__ = ("name", "w", "r", "excl")

    def __init__(self, name, excl=False):
        self.name = name
        self.w = {}
        self.r = {}
        self.excl = excl


class Eng:
    def __init__(self, name, handle, sem, self_sync=True):
        self.name = name
        self.h = handle
        self.sem = sem
        self.count = 0
        self.waited = {}
        self.self_sync = self_sync


class DSem:
    def __init__(self, sem):
        self.sem = sem
        self.count = 0


def _merge(d, src):
    for k, v in src.items():
        if d.get(k, (None, 0))[1] < v[1]:
            d[k] = v


class Tracker:
    def deps_for(self, reads, writes):
        deps = {}
        for b in reads:
            _merge(deps, b.w)
            if b.excl:
                _merge(deps, b.r)
        for b in writes:
            _merge(deps, b.w)
            _merge(deps, b.r)
        return deps

    def wait(self, eng, deps):
        for key, (sem, val) in deps.items():
            if key == id(eng.sem) and not eng.self_sync:
                continue
            if eng.waited.get(key, 0) >= val:
                continue
            eng.h.wait_ge(sem, val)
            eng.waited[key] = val

    def record(self, ms_key, ms, reads, writes):
        for b in reads:
            if b.excl:
                b.w = {ms_key: ms}
                b.r = {}
            else:
                if b.r.get(ms_key, (None, 0))[1] < ms[1]:
                    b.r[ms_key] = ms
        for b in writes:
            b.w = {ms_key: ms}
            b.r = {}

    def op(self, eng, fn, reads=(), writes=()):
        self.wait(eng, self.deps_for(reads, writes))
        ins = fn(eng.h)
        eng.count += 1
        ins.then_inc(eng.sem, 1)
        self.record(id(eng.sem), (eng.sem, eng.count), reads, writes)

    def dma(self, eng, fn, dsem, reads=(), writes=()):
        self.wait(eng, self.deps_for(reads, writes))
        inss = fn(eng.h)
        for ins in inss:
            ins.then_inc(dsem.sem, 16)
            dsem.count += 16
        self.record(id(dsem.sem), (dsem.sem, dsem.count), reads, writes)


def inherit(new_bufs, old_bufs):
    for nb_ in new_bufs:
        for ob in old_bufs:
            _merge(nb_.r, ob.w)
            _merge(nb_.r, ob.r)


def build_program():
    nc = bass.Bass("TRN2", target_bir_lowering=False)
    dt = nc.dram_tensor
    xT = dt("xT", [D, S], F32, kind="ExternalInput").ap()
    w1i = dt("w1i", [D, 2 * DFF], F32, kind="ExternalInput").ap()
    w1o = dt("w1o", [DFF, D], F32, kind="ExternalInput").ap()
    wi = dt("wi", [D, 7168], F32, kind="ExternalInput").ap()
    pw = dt("pw", [4, 256, 256], F32, kind="ExternalInput").ap()
    wru = dt("wru", [D, D], F32, kind="ExternalInput").ap()
    wpu = dt("wpu", [D, D], F32, kind="ExternalInput").ap()
    wo = dt("wo", [D, D], F32, kind="ExternalInput").ap()
    w2i = dt("w2i", [D, 2 * DFF], F32, kind="ExternalInput").ap()
    w2o = dt("w2o", [DFF, D], F32, kind="ExternalInput").ap()
    consts_d = dt("consts", [128, NCONST], F32, kind="ExternalInput").ap()
    cset_d = dt("cset", [128, NSET], F32, kind="ExternalInput").ap()
    vecs_d = dt("vecs", [128, NVEC], F32, kind="ExternalInput").ap()
    outT = dt("outT", [D, S], F32, kind="ExternalOutput").ap()

    def kview(w):
        return w.rearrange("(kc p) n -> p kc n", p=128)

    w1i_v, w1o_v, wi_v, wru_v, wpu_v, wo_v, w2i_v, w2o_v = map(kview, (w1i, w1o, wi, wru, wpu, wo, w2i, w2o))

    sb = nc.alloc_sbuf_tensor
    h_t = sb("h", [128, 2 * KC * T], F32)
    xn_t = sb("xn", [128, KC * T], BF16)
    ring_t = sb("ring", [128, NSLOT * SLOT], BF16)
    A_QT = 0
    A_KT = A_QT + KC * T
    A_KTOK = A_KT + KC * T
    A_VTOK = A_KTOK + 4 * 1024
    A_SGR = A_VTOK + 4 * 1024
    A_POOLED = A_SGR + KC * T
    A_POOLOUT = A_POOLED + KC * T
    A_QD = A_POOLOUT + KC * T
    A_SC = A_QD + KC * T
    SC_W = 1280
    A_END = A_SC + 2 * SC_W
    arena_t = sb("arena", [128, A_END], BF16)
    A_HID = 0
    A_MERGED = A_QT
    A_RETN = A_POOLED
    assert HC * T <= A_END
    sq_t = sb("sq", [128, 2 * T], BF16)
    rstd_t = sb("rstd", [128, T], F32)
    std_t = sb("std", [128, T], F32)
    acc_t = std_t
    sg_t = sb("sg", [128, 3 * T], F32)
    tmp_t = sb("tmp", [128, 2 * T], F32)
    pbuf_t = sb("pbuf", [128, 2 * PL], F32)
    pa_t = sb("pwin", [128, 2 * PL], F32)
    halo_t = sb("halo", [128, KC * HALO], F32)
    s32_t = sb("s32", [128, 4 * 512], F32)
    sbf_t = sb("sbf", [128, 4 * 512], BF16)
    cos_t = sb("cos", [128, T], F32)
    sin_t = sb("sin", [128, T], F32)
    ang_t = sb("ang", [128, 3 * T], F32)
    dq_t = sb("dq", [128, 4 * T], F32)
    mask_t = sb("mask", [128, 4 * 512], F32)
    consts_t = sb("constsb", [128, NCONST], F32)
    vecs_t = sb("vecsb", [128, NVEC], F32)
    gbh_t = sb("gbh", [128, 16], F32)
    ksc_t = sb("ksc", [128, 16], F32)
    invc_t = sb("invc", [128, 64], F32)
    ones_t = sb("ones", [128, 128], BF16)
    ident_t = sb("ident", [128, 128], BF16)
    epsc_t = sb("epsc", [128, 4], F32)
    banks_t = [nc.alloc_psum_tensor(f"bank{i}", [128, 512], F32) for i in range(8)]

    tr = Tracker()
    from contextlib import ExitStack

    with ExitStack() as es:
        def sem(name):
            return es.enter_context(nc.semaphore(name))

        PE = Eng("pe", nc.tensor, sem("s_pe"), self_sync=False)
        ACT = Eng("act", nc.scalar, sem("s_act"))
        DVE = Eng("dve", nc.vector, sem("s_dve"))
        POOLQ = Eng("poolq", nc.gpsimd, sem("s_pool"))
        SP = Eng("sp", nc.sync, sem("s_sp"))

        ring_b = [Buf(f"ring{i}") for i in range(NSLOT)]
        ring_s = [DSem(sem(f"s_ring{i}")) for i in range(NSLOT)]
        h_b2 = [[Buf(f"h{q}_{k}") for k in range(KC)] for q in range(2)]
        h_s2 = [DSem(sem("s_hld0")), DSem(sem("s_hld1"))]
        xn_b = [Buf(f"xn{k}") for k in range(KC)]
        hid_b = [Buf(f"hid{j}") for j in range(HC)]
        qT_b = [Buf(f"qT{k}") for k in range(KC)]
        kT_b = [Buf(f"kT{k}") for k in range(KC)]
        ktok_b = [Buf(f"ktok{k}") for k in range(4)]
        vtok_b = [Buf(f"vtok{k}") for k in range(4)]
        sgr_b = [Buf(f"sgr{k}") for k in range(KC)]
        pooled_b = [Buf(f"pooled{k}") for k in range(KC)]
        poolout_b = [Buf(f"poolout{k}") for k in range(KC)]
        merged_b = [Buf(f"merged{k}") for k in range(KC)]
        retn_b = [Buf(f"retn{k}") for k in range(KC)]
        qd_b = [Buf(f"qd{k}") for k in range(KC)]
        sc_b = [Buf(f"sc{r}") for r in range(2)]
        sq_b = [Buf(f"sq{r}") for r in range(2)]
        rstd_b = Buf("rstd")
        std_b = Buf("std")
        acc_b = std_b
        sg_b = [Buf(f"sg{r}") for r in range(3)]
        tmp_b = [Buf(f"tmp{r}") for r in range(2)]
        pbuf_b = [Buf(f"pbuf{r}") for r in range(2)]
        pa_b = [Buf(f"pa{r}") for r in range(2)]
        halo_b = [Buf(f"halo{k}") for k in range(KC)]
        s32_b = [Buf(f"s32{k}") for k in range(4)]
        sbf_b = [Buf(f"sbf{k}") for k in range(4)]
        cos_b = Buf("cos")
        sin_b = Buf("sin")
        ang_b = [Buf("ang0"), Buf("ang1"), Buf("ang2")]
        dq_b = Buf("dq")
        mask_b = Buf("mask")
        ob_s = [DSem(sem("s_ob0")), DSem(sem("s_ob1"))]
        const_b = Buf("consts")
        cset_b = Buf("cset")
        const_s = DSem(sem("s_const"))
        cset_s = DSem(sem("s_cset"))
        small_b = Buf("small")
        bank_b = [Buf(f"bank{i}", excl=True) for i in range(8)]

        mixer_alias = (qT_b + kT_b + ktok_b + vtok_b + sgr_b + pooled_b + poolout_b + merged_b + retn_b
                       + qd_b + sc_b)

        hsel = [0]

        def hA(k, a=0, b=T):
            o = hsel[0] * KC * T
            return h_t[:, o + k * T + a:o + k * T + b]

        def hB():
            return h_b2[hsel[0]]

        def hAq(q):
            def f(k, a=0, b=T):
                o = q * KC * T
                return h_t[:, o + k * T + a:o + k * T + b]
            return f

        NORMBANK = 7
        bg = []

        def bg_step():
            if bg:
                bg.pop(0)()

        def bg_flush():
            while bg:
                bg.pop(0)()

        cset_v = arena_t[:, 0:2 * NSET].bitcast(F32)

        def cs_(a, b):
            return cset_v[:, a:b]

        def xnA(k, a=0, b=T):
            return xn_t[:, k * T + a:k * T + b]

        def ar(off, k, a=0, b=T, stride=T):
            return arena_t[:, off + k * stride + a: off + k * stride + b]

        def hidA(j):
            return ar(A_HID, j)

        def qTA(k, a=0, b=T):
            return ar(A_QT, k, a, b)

        def kTA(k, a=0, b=T):
            return ar(A_KT, k, a, b)

        def ktokA(tc, a, b):
            return ar(A_KTOK, tc, a, b, 1024)

        def vtokA(tc, a, b):
            return ar(A_VTOK, tc, a, b, 1024)

        def bankA(i, a=0, b=512):
            return banks_t[i][:, a:b]

        def bankBF(i, a, b):
            return banks_t[i][:, :].bitcast(BF16)[:, a:b]

        def cst(a, b):
            return consts_t[:, a:b]

        bank_rr = [0]

        def nb():
            i = bank_rr[0]
            bank_rr[0] = (i + 1) % 7
            return i

        rr = {}

        def rot(name, n):
            v = rr.get(name, 0)
            rr[name] = (v + 1) % n
            return v

        slab_loaders = []
        st = {"next_load": 0, "cur": 0}

        def slot_view3(slot, kc, w):
            return ring_t[:, slot * SLOT: slot * SLOT + kc * w].rearrange("p (k w) -> p k w", w=w)

        def ld_ffn_in(wv):
            def mk(s):
                def f(e, slot):
                    v = slot_view3(slot, KC, 512)
                    return [e.dma_start(out=v[:, :, 0:256], in_=wv[:, :, 256 * s:256 * s + 256]),
                            e.dma_start(out=v[:, :, 256:512], in_=wv[:, :, DFF + 256 * s:DFF + 256 * s + 256])]
                return f
            return [mk(s) for s in range(HC // 2)]

        def ld_ffn_out(wv):
            def mk(m):
                def f(e, slot):
                    v = slot_view3(slot, HC, 128)
                    return [e.dma_start(out=v, in_=wv[:, :, 128 * m:128 * m + 128])]
                return f
            return [mk(m) for m in range(KC)]

        def ld_cols(wv, c0):
            def f(e, slot):
                v = slot_view3(slot, KC, 512)
                return [e.dma_start(out=v, in_=wv[:, :, c0:c0 + 512])]
            return f

        def ld_poolw(e, slot):
            out = []
            for g in range(4):
                v = ring_t[:, slot * SLOT + g * 512: slot * SLOT + (g + 1) * 512].rearrange("p (k w) -> p k w", w=256)
                out.append(e.dma_start(out=v, in_=pw[g].rearrange("(kc p) d -> p kc d", p=128)))
            return out

        def ld_merge(m):
            def f(e, slot):
                v = slot_view3(slot, KC, 512)
                return [e.dma_start(out=v[:, :, 0:128], in_=wi_v[:, :, 5120 + 128 * m:5120 + 128 * m + 128]),
                        e.dma_start(out=v[:, :, 128:256], in_=wi_v[:, :, 6144 + 128 * m:6144 + 128 * m + 128]),
                        e.dma_start(out=v[:, :, 256:384], in_=wru_v[:, :, 128 * m:128 * m + 128]),
                        e.dma_start(out=v[:, :, 384:512], in_=wpu_v[:, :, 128 * m:128 * m + 128])]
            return f

        tile_loaders = (ld_ffn_in(w1i_v) + ld_ffn_out(w1o_v) + [ld_cols(wi_v, 512 * s) for s in range(10)]
                        + [ld_poolw] + [ld_merge(m) for m in range(KC)] + [ld_cols(wo_v, 512 * s) for s in range(2)]
                        + ld_ffn_in(w2i_v) + ld_ffn_out(w2o_v))
        NSLAB = len(tile_loaders)
        for _ in range(NT):
            slab_loaders.extend(tile_loaders)

        def ensure_loaded(upto):
            while st["next_load"] <= upto and st["next_load"] < len(slab_loaders):
                i = st["next_load"]
                slot = i % NSLOT
                f = slab_loaders[i]
                tr.dma(POOLQ, lambda e, f=f, slot=slot: f(e, slot), ring_s[slot], writes=[ring_b[slot]])
                st["next_load"] += 1

        def acquire():
            i = st["cur"]
            ensure_loaded(i)
            return i % NSLOT

        def release():
            st["cur"] += 1
            ensure_loaded(st["cur"] + NSLOT - 1)

        def slab(slot, a, b):
            return ring_t[:, slot * SLOT + a: slot * SLOT + b]

        ensure_loaded(NSLOT - 1)
        tr.dma(SP, lambda e: [e.dma_start(out=consts_t[:, :], in_=consts_d),
                              e.dma_start(out=vecs_t[:, :], in_=vecs_d)], const_s, writes=[const_b])
        tr.dma(SP, lambda e: [e.dma_start(out=cset_v, in_=cset_d)], cset_s, writes=[cset_b])
        tr.op(DVE, lambda e: e.memset(ones_t[:, :], 1.0), writes=[small_b])
        tr.op(DVE, lambda e: e.memset(s32_t[:, :], 0.0), writes=s32_b)
        tr.op(DVE, lambda e: e.memset(sbf_t[:, :], 0.0), writes=sbf_b)
        tr.op(DVE, lambda e: e.memset(halo_t[:, :], 0.0), writes=halo_b)
        tr.op(DVE, lambda e: e.memset(epsc_t[:, 0:1], EPS), writes=[small_b])
        tr.op(DVE, lambda e: e.memset(epsc_t[:, 1:2], LN16), reads=[small_b], writes=[small_b])
        tr.op(DVE, lambda e: e.tensor_copy(ident_t[:, :], cst(C_ID, C_ID + 128)), reads=[const_b, small_b], writes=[small_b])
        tr.op(DVE, lambda e: e.tensor_scalar_mul(gbh_t[:, :], vecs_t[:, V_GB0:V_GB0 + 16], 0.5),
              reads=[const_b, small_b], writes=[small_b])
        tr.op(DVE, lambda e: e.reciprocal(invc_t[:, :], cs_(S_CNT, S_CNT + 64)), reads=[cset_b, small_b], writes=[small_b])
        tr.op(DVE, lambda e: e.tensor_scalar_add(ang_t[:, 0:T], cst(C_TLOC, C_TLOC + T), 1.0), reads=[const_b], writes=[ang_b[0]])
        for hh in range(4):
            tr.op(ACT, lambda e, hh=hh: e.activation(dq_t[:, hh * T:(hh + 1) * T], ang_t[:, 0:T], AF.Exp, scale=LG[hh]),
                  reads=[ang_b[0]], writes=[dq_b])
            tr.op(ACT, lambda e, hh=hh: e.activation(mask_t[:, hh * 512:(hh + 1) * 512], cs_(S_DIST, S_DIST + 512), AF.Exp,
                                                     scale=LG[hh], bias=epsc_t[:, 1:2]),
                  reads=[cset_b, small_b], writes=[mask_b])
            tr.op(ACT, lambda e, hh=hh: e.activation(ksc_t[:, hh * 4:(hh + 1) * 4], cs_(S_KREV, S_KREV + 4), AF.Exp,
                                                     scale=LG[hh], bias=epsc_t[:, 1:2]),
                  reads=[cset_b, small_b], writes=[small_b])
        for hh in range(4):
            tr.op(DVE, lambda e, hh=hh: e.tensor_tensor(mask_t[:, hh * 512:hh * 512 + 128], mask_t[:, hh * 512:hh * 512 + 128],
                                                        cs_(S_VALID, S_VALID + 128), ALU.mult),
                  reads=[cset_b, mask_b], writes=[mask_b])
        inherit(hid_b + mixer_alias, [cset_b])

        def pe_group(bank, mms, reads):
            tr.wait(PE, tr.deps_for(reads, [bank_b[bank]]))
            ins = None
            n = len(mms)
            extra = []
            for i, mm in enumerate(mms):
                if len(mm) == 4:
                    tr.wait(PE, tr.deps_for(mm[3], []))
                    extra.extend(mm[3])
                ins = PE.h.matmul(mm[0], mm[1], mm[2], start=(i == 0), stop=(i == n - 1))
            PE.count += 1
            ins.then_inc(PE.sem, 1)
            tr.record(id(PE.sem), (PE.sem, PE.count), list(reads) + extra, [bank_b[bank]])

        def norm_stats(src_b, srcA, nchunks, inv_n, pn=None):
            if pn is None:
                pn = nb()
            for kc in range(nchunks):
                r = rot("sq", 2)
                tr.op(ACT, lambda e, kc=kc, r=r: e.activation(sq_t[:, r * T:(r + 1) * T], srcA(kc), AF.Square),
                      reads=[src_b[kc]], writes=[sq_b[r]])
                tr.op(PE, lambda e, kc=kc, r=r: e.matmul(bankA(pn), ones_t[:, :], sq_t[:, r * T:(r + 1) * T],
                                                         start=(kc == 0), stop=(kc == nchunks - 1)),
                      reads=[sq_b[r], small_b], writes=[bank_b[pn]])
            rstd_from(pn, inv_n)

        def rstd_from(pn, inv_n):
            tr.op(ACT, lambda e: e.activation(std_t[:, :], bankA(pn), AF.Ln, bias=epsc_t[:, 0:1], scale=inv_n),
                  reads=[bank_b[pn], small_b], writes=[std_b])
            tr.op(ACT, lambda e: e.activation(rstd_t[:, :], std_t[:, :], AF.Exp, scale=-0.5), reads=[std_b], writes=[rstd_b])

        def norm_step(src_b, srcA, kc):
            r = rot("sq", 2)
            tr.op(ACT, lambda e: e.activation(sq_t[:, r * T:(r + 1) * T], srcA(kc), AF.Square),
                  reads=[src_b[kc]], writes=[sq_b[r]])
            if kc == 0:
                return
            if kc == 1:
                tr.op(DVE, lambda e: e.tensor_tensor(acc_t[:, :], sq_t[:, 0:T], sq_t[:, T:2 * T], ALU.add),
                      reads=[sq_b[0], sq_b[1]], writes=[acc_b])
            elif kc < KC - 1:
                tr.op(DVE, lambda e: e.tensor_tensor(acc_t[:, :], acc_t[:, :], sq_t[:, r * T:(r + 1) * T], ALU.add),
                      reads=[acc_b, sq_b[r]], writes=[acc_b])
            else:
                assert r == 1
                tr.op(DVE, lambda e: e.tensor_tensor(sq_t[:, 0:T], acc_t[:, :], sq_t[:, T:2 * T], ALU.add),
                      reads=[acc_b, sq_b[1], sq_b[0]], writes=[sq_b[0]])

        def norm_finish(inv_n, pn=None):
            if pn is None:
                pn = nb()
            tr.op(PE, lambda e: e.matmul(bankA(pn), ones_t[:, :], sq_t[:, 0:T], start=True, stop=True),
                  reads=[sq_b[0], small_b], writes=[bank_b[pn]])
            rstd_from(pn, inv_n)

        def rmsnorm_to_xn(gcol, stats_done=False):
            hb = hB()
            if not stats_done:
                rr["sq"] = 0
                for kc in range(KC):
                    norm_step(hb, hA, kc)
            norm_finish(1.0 / D)
            for kc in range(KC):
                tr.op(DVE, lambda e, kc=kc: e.scalar_tensor_tensor(xnA(kc), hA(kc), vecs_t[:, gcol + kc:gcol + kc + 1],
                                                                   rstd_t[:, :], ALU.mult, ALU.mult),
                      reads=[hb[kc], rstd_b, const_b], writes=[xn_b[kc]])

        def xn_mms(bank, slot, col):
            return [(bankA(bank), slab(slot, kc * 512 + col, kc * 512 + col + 128), xnA(kc), [xn_b[kc]]) for kc in range(KC)]

        def ffn(between=None, post_res=None):
            hb = hB()
            for s in range(HC // 2):
                slot = acquire()
                for jj in range(2):
                    j = 2 * s + jj
                    pg, pu = nb(), nb()
                    pe_group(pg, xn_mms(pg, slot, jj * 128), reads=[ring_b[slot]])
                    pe_group(pu, xn_mms(pu, slot, 256 + jj * 128), reads=[ring_b[slot]])
                    r = rot("sg", 3)
                    tr.op(ACT, lambda e, r=r, pg=pg: e.activation(sg_t[:, r * T:(r + 1) * T], bankA(pg), AF.Silu),
                          reads=[bank_b[pg]], writes=[sg_b[r]])
                    tr.op(DVE, lambda e, r=r, pu=pu, j=j: e.tensor_tensor(hidA(j), bankA(pu), sg_t[:, r * T:(r + 1) * T], ALU.mult),
                          reads=[bank_b[pu], sg_b[r]], writes=[hid_b[j]])
                    bg_step()
                release()
            if between is not None:
                between()
            bg_flush()
            if post_res is not None:
                rr["sq"] = 0
            for m in range(KC):
                slot = acquire()
                po = nb()
                pe_group(po, [(bankA(po), slab(slot, kc * 128, kc * 128 + 128), hidA(kc), [hid_b[kc]]) for kc in range(HC)],
                         reads=[ring_b[slot]])
                tr.op(DVE, lambda e, m=m, po=po: e.scalar_tensor_tensor(hA(m), bankA(po), 0.5, hA(m), ALU.mult, ALU.add),
                      reads=[bank_b[po], hb[m]], writes=[hb[m]])
                if post_res is not None:
                    post_res(m)
                bg_step()
                release()

        def rotary(pa, pb, dst_b, dstA, c0):
            t0, t1 = 0, 1
            tr.op(DVE, lambda e: e.tensor_tensor(tmp_t[:, 0:T], bankA(pa), cos_t[:, :], ALU.mult),
                  reads=[bank_b[pa], cos_b], writes=[tmp_b[0]])
            tr.op(DVE, lambda e: e.tensor_tensor(tmp_t[:, T:2 * T], bankA(pb), sin_t[:, :], ALU.mult),
                  reads=[bank_b[pb], sin_b], writes=[tmp_b[1]])
            tr.op(DVE, lambda e: e.tensor_tensor(dstA(c0), tmp_t[:, 0:T], tmp_t[:, T:2 * T], ALU.subtract),
                  reads=[tmp_b[0], tmp_b[1]], writes=[dst_b[c0]])
            tr.op(DVE, lambda e: e.tensor_tensor(tmp_t[:, 0:T], bankA(pa), sin_t[:, :], ALU.mult),
                  reads=[bank_b[pa], sin_b], writes=[tmp_b[0]])
            tr.op(DVE, lambda e: e.tensor_tensor(tmp_t[:, T:2 * T], bankA(pb), cos_t[:, :], ALU.mult),
                  reads=[bank_b[pb], cos_b], writes=[tmp_b[1]])
            tr.op(DVE, lambda e: e.tensor_tensor(dstA(c0 + 1), tmp_t[:, 0:T], tmp_t[:, T:2 * T], ALU.add),
                  reads=[tmp_b[0], tmp_b[1]], writes=[dst_b[c0 + 1]])

        def gen_rope_tables(ti):
            off = float(ti * T)
            MAGIC = 12582912.0
            TWO_PI = 2.0 * PI
            PI_LO = 3.1415925

            def A(i):
                return ang_t[:, i * T:(i + 1) * T]
            tr.op(DVE, lambda e: e.tensor_scalar_add(A(0), cst(C_TLOC, C_TLOC + T), off), reads=[const_b], writes=[ang_b[0]])
            tr.op(DVE, lambda e: e.tensor_scalar_mul(A(0), A(0), cst(C_INVF, C_INVF + 1)), reads=[const_b, ang_b[0]], writes=[ang_b[0]])

            def reduce_sin(src_i, out_t, out_b):
                tr.op(DVE, lambda e: e.tensor_scalar(A(1), A(src_i), 1.0 / TWO_PI, MAGIC, ALU.mult, ALU.add),
                      reads=[ang_b[src_i]], writes=[ang_b[1]])
                tr.op(DVE, lambda e: e.tensor_scalar(A(1), A(1), MAGIC, TWO_PI, ALU.subtract, ALU.mult),
                      reads=[ang_b[1]], writes=[ang_b[1]])
                tr.op(DVE, lambda e: e.tensor_tensor(A(2), A(src_i), A(1), ALU.subtract),
                      reads=[ang_b[src_i], ang_b[1]], writes=[ang_b[2]])
                tr.op(DVE, lambda e: e.tensor_scalar(A(2), A(2), PI_LO, -PI_LO, ALU.min, ALU.max),
                      reads=[ang_b[2]], writes=[ang_b[2]])
                tr.op(ACT, lambda e: e.activation(out_t[:, :], A(2), AF.Sin), reads=[ang_b[2]], writes=[out_b])
            reduce_sin(0, sin_t, sin_b)
            tr.op(DVE, lambda e: e.tensor_scalar_add(A(2), A(0), PI / 2), reads=[ang_b[0], ang_b[2]], writes=[ang_b[2]])
            reduce_sin(2, cos_t, cos_b)

        def pooling(ti, m, pp):
            g = m // 2
            w = 2 ** (g + 1)
            r = rot("pbuf", 2)
            base = r * PL

            def pbA(a, b):
                return pbuf_t[:, base + a: base + b]

            tr.op(ACT, lambda e: e.activation(pbA(HALO, PL), bankA(pp), AF.Copy), reads=[bank_b[pp]], writes=[pbuf_b[r]])
            tr.op(ACT, lambda e: e.activation(pbA(0, HALO), halo_t[:, m * HALO:(m + 1) * HALO], AF.Copy),
                  reads=[halo_b[m], pbuf_b[r]], writes=[pbuf_b[r]])
            tr.op(ACT, lambda e: e.activation(halo_t[:, m * HALO:(m + 1) * HALO], pbA(T, PL), AF.Copy),
                  reads=[pbuf_b[r]], writes=[halo_b[m]])

            def paA(q, a, b):
                return pa_t[:, q * PL + a: q * PL + b]

            cur = None
            steps = []
            k = 1
            q = 0
            while k < w:
                steps.append((k, q))
                k *= 2
                q ^= 1
            srcA, src_buf = pbA, pbuf_b[r]
            lo = 0
            for (k, q) in steps:
                lo2 = lo + k
                dA = (lambda q: (lambda a, b: paA(q, a, b)))(q)
                tr.op(DVE, lambda e, srcA=srcA, dA=dA, lo2=lo2, k=k: e.tensor_tensor(dA(lo2, PL), srcA(lo2, PL), srcA(lo2 - k, PL - k), ALU.add),
                      reads=[src_buf], writes=[pa_b[q]])
                srcA, src_buf, lo = dA, pa_b[q], lo2
            tr.op(DVE, lambda e, srcA=srcA: e.scalar_tensor_tensor(ar(A_POOLED, m), srcA(HALO, PL), 1.0 / w, pbA(HALO, PL),
                                                                     ALU.mult, ALU.subtract),
                  reads=[src_buf, pbuf_b[r]], writes=[pooled_b[m]])
            if ti == 0:
                tr.op(DVE, lambda e, srcA=srcA: e.tensor_tensor(tmp_t[:, 0:HALO], srcA(HALO, 2 * HALO), invc_t[:, g * 16:(g + 1) * 16], ALU.mult),
                      reads=[src_buf, small_b], writes=[tmp_b[0]])
                tr.op(DVE, lambda e: e.tensor_tensor(ar(A_POOLED, m, 0, HALO), tmp_t[:, 0:HALO], pbA(HALO, 2 * HALO), ALU.subtract),
                      reads=[tmp_b[0], pbuf_b[r], pooled_b[m]], writes=[pooled_b[m]])

        def mixer(ti):
            hb = hB()
            rmsnorm_to_xn(V_GM, stats_done=True)
            inherit(mixer_alias, hid_b)
            for which, dst_b, dstA in ((0, qT_b, qTA), (1, kT_b, kTA)):
                for s in range(2):
                    slot = acquire()
                    for hh in range(2):
                        head = 2 * s + hh
                        pa, pb = nb(), nb()
                        pe_group(pa, xn_mms(pa, slot, hh * 256), reads=[ring_b[slot]])
                        pe_group(pb, xn_mms(pb, slot, hh * 256 + 128), reads=[ring_b[slot]])
                        rotary(pa, pb, dst_b, dstA, 2 * head)
                    release()
            for c in range(KC):
                tr.op(DVE, lambda e: e.tensor_tensor(arena_t[:, A_QD + c * T: A_QD + (c + 1) * T], qTA(c),
                                                     dq_t[:, (c // 2) * T:(c // 2 + 1) * T], ALU.mult),
                      reads=[qT_b[c], dq_b], writes=[qd_b[c]])
            for s in range(2):
                slot = acquire()
                for tc in range(4):
                    pv = nb()
                    pe_group(pv, [(bankA(pv), xnA(kc, tc * 128, (tc + 1) * 128), slab(slot, kc * 512, (kc + 1) * 512))
                                  for kc in range(KC)], reads=xn_b + [ring_b[slot]])
                    tr.op(ACT, lambda e, tc=tc, s=s, pv=pv: e.activation(vtokA(tc, s * 512, (s + 1) * 512), bankA(pv), AF.Copy),
                          reads=[bank_b[pv]], writes=[vtok_b[tc]])
                release()
            for s in range(2):
                slot = acquire()
                for mm in range(4):
                    m = 4 * s + mm
                    pg = nb()
                    pe_group(pg, xn_mms(pg, slot, mm * 128), reads=[ring_b[slot]])
                    tr.op(ACT, lambda e, m=m, pg=pg: e.activation(ar(A_SGR, m), bankA(pg), AF.Silu),
                          reads=[bank_b[pg]], writes=[sgr_b[m]])
                release()
            for s in range(2):
                slot = acquire()
                for mm in range(4):
                    m = 4 * s + mm
                    pp = nb()
                    pe_group(pp, xn_mms(pp, slot, mm * 128), reads=[ring_b[slot]])
                    pooling(ti, m, pp)
                release()
            slot = acquire()
            for g in range(4):
                for mm in range(2):
                    po = nb()
                    pe_group(po, [(bankA(po), slab(slot, g * 512 + kc * 256 + mm * 128, g * 512 + kc * 256 + mm * 128 + 128),
                                   ar(A_POOLED, 2 * g + kc)) for kc in range(2)],
                             reads=[pooled_b[2 * g], pooled_b[2 * g + 1], ring_b[slot]])
                    c = 2 * g + mm
                    tr.op(ACT, lambda e, c=c, po=po: e.activation(ar(A_POOLOUT, c), bankA(po), AF.Identity,
                                                                  scale=vecs_t[:, V_PS + c:V_PS + c + 1]),
                          reads=[bank_b[po], const_b], writes=[poolout_b[c]])
            release()
            for tc in range(4):
                pt = nb()

                def ftr(e, tc=tc, pt=pt):
                    ins = None
                    for f in range(KC):
                        ins = e.transpose(bankBF(pt, f * 128, (f + 1) * 128), kTA(f, tc * 128, (tc + 1) * 128), ident_t[:, :])
                    return ins
                tr.op(PE, ftr, reads=kT_b + [small_b], writes=[bank_b[pt]])
                for head in range(4):
                    tr.op(ACT, lambda e, tc=tc, pt=pt, head=head: e.activation(
                        ktokA(tc, head * 256, (head + 1) * 256), bankBF(pt, head * 256, (head + 1) * 256), AF.Identity,
                        scale=ksc_t[:, head * 4 + tc:head * 4 + tc + 1]),
                        reads=[bank_b[pt], small_b], writes=[ktok_b[tc]])
            inherit(retn_b, pooled_b)
            offs = [0, 512, 896, 1152]
            hstate = {}

            def scA(r, sbk, n):
                return arena_t[:, A_SC + r * SC_W + offs[sbk]: A_SC + r * SC_W + offs[sbk] + n]

            def qdA(c):
                return arena_t[:, A_QD + c * T: A_QD + (c + 1) * T]

            def stage_S(head):
                r = rot("sc", 2)
                for sbk in range(4):
                    n = (4 - sbk) * 128
                    psc = rot("scbank", 2)
                    pe_group(psc, [(bankA(psc, 0, n), kTA(2 * head + i, sbk * 128, (sbk + 1) * 128), qTA(2 * head + i, sbk * 128, T))
                                   for i in range(2)], reads=[kT_b[2 * head], kT_b[2 * head + 1], qT_b[2 * head], qT_b[2 * head + 1]])
                    tr.op(DVE, lambda e: e.tensor_tensor(scA(r, sbk, n), bankA(psc, 0, n), mask_t[:, head * 512: head * 512 + n], ALU.mult),
                          reads=[bank_b[psc], mask_b], writes=[sc_b[r]])
                hstate[head] = r

            def stage_O(head):
                r = hstate[head]
                pos = [2, 3] if head % 2 == 0 else [4, 5]
                pn = 6
                for ec in range(2):
                    po = pos[ec]
                    mms = []
                    for sbk in range(4):
                        n = (4 - sbk) * 128
                        mms.append((bankA(po, sbk * 128, T), vtokA(sbk, head * 256 + ec * 128, head * 256 + ec * 128 + 128),
                                    scA(r, sbk, n)))
                    for i in range(2):
                        mms.append((bankA(po), sbf_t[:, head * 512 + i * 256 + ec * 128: head * 512 + i * 256 + ec * 128 + 128],
                                    qdA(2 * head + i)))
                    pe_group(po, mms, reads=vtok_b + [sc_b[r], sbf_b[head], qd_b[2 * head], qd_b[2 * head + 1]])
                    rs = rot("sq", 2)
                    tr.op(ACT, lambda e: e.activation(sq_t[:, rs * T:(rs + 1) * T], bankA(po), AF.Square),
                          reads=[bank_b[po]], writes=[sq_b[rs]])
                    tr.op(PE, lambda e: e.matmul(bankA(pn), ones_t[:, :], sq_t[:, rs * T:(rs + 1) * T],
                                                 start=(ec == 0), stop=(ec == 1)),
                          reads=[sq_b[rs], small_b], writes=[bank_b[pn]])

            def stage_F(head):
                pos = [2, 3] if head % 2 == 0 else [4, 5]
                rstd_from(6, 1.0 / 256)
                for ec in range(2):
                    c = 2 * head + ec
                    rt = rot("tmp", 2)
                    tr.op(DVE, lambda e: e.tensor_tensor(tmp_t[:, rt * T:(rt + 1) * T], bankA(pos[ec]), rstd_t[:, :], ALU.mult),
                          reads=[bank_b[pos[ec]], rstd_b], writes=[tmp_b[rt]])
                    tr.op(DVE, lambda e: e.tensor_tensor(ar(A_RETN, c), tmp_t[:, rt * T:(rt + 1) * T], ar(A_SGR, c), ALU.mult),
                          reads=[tmp_b[rt], sgr_b[c]], writes=[retn_b[c]])
                pS = 7
                mms = []
                for i in range(2):
                    for tc in range(4):
                        mms.append((bankA(pS, i * 256, (i + 1) * 256), ktokA(tc, head * 256 + i * 128, head * 256 + i * 128 + 128),
                                    vtokA(tc, head * 256, head * 256 + 256)))

                def fst(e):
                    ins = None
                    for idx, (o, l, rr_) in enumerate(mms):
                        ins = e.matmul(o, l, rr_, start=(idx % 4 == 0), stop=(idx % 4 == 3))
                    return ins
                tr.op(PE, fst, reads=ktok_b + vtok_b, writes=[bank_b[pS]])
                cd = math.exp(LG[head] * T)
                tr.op(DVE, lambda e: e.scalar_tensor_tensor(s32_t[:, head * 512:(head + 1) * 512],
                                                            s32_t[:, head * 512:(head + 1) * 512], cd, bankA(pS),
                                                            ALU.mult, ALU.add),
                      reads=[bank_b[pS], s32_b[head]], writes=[s32_b[head]])
                tr.op(ACT, lambda e: e.activation(sbf_t[:, head * 512:(head + 1) * 512], s32_t[:, head * 512:(head + 1) * 512], AF.Copy),
                      reads=[s32_b[head]], writes=[sbf_b[head]])

            if os.environ.get("KDBG_SEQRET"):
                for hd in range(4):
                    stage_S(hd)
                    stage_O(hd)
                    stage_F(hd)
            else:
                stage_S(0)
                stage_S(1)
                stage_O(0)
                stage_S(2)
                stage_F(0)
                stage_O(1)
                stage_S(3)
                stage_F(1)
                stage_O(2)
                stage_F(2)
                stage_O(3)
                stage_F(3)
            inherit(merged_b, qT_b + kT_b + ktok_b)
            for m in range(KC):
                slot = acquire()
                pg0, pg1, pr, pp = nb(), nb(), nb(), nb()
                pe_group(pg0, xn_mms(pg0, slot, 0), reads=[ring_b[slot]])
                pe_group(pg1, xn_mms(pg1, slot, 128), reads=[ring_b[slot]])
                pe_group(pr, [(bankA(pr), slab(slot, kc * 512 + 256, kc * 512 + 384), ar(A_RETN, kc)) for kc in range(KC)],
                         reads=retn_b + [ring_b[slot]])
                pe_group(pp, [(bankA(pp), slab(slot, kc * 512 + 384, kc * 512 + 512), ar(A_POOLOUT, kc)) for kc in range(KC)],
                         reads=poolout_b + [ring_b[slot]])
                r0, r1 = rot("sg", 3), rot("sg", 3)
                tr.op(ACT, lambda e, r0=r0: e.activation(sg_t[:, r0 * T:(r0 + 1) * T], bankA(pg0), AF.Tanh, scale=0.5,
                                                         bias=gbh_t[:, m:m + 1]),
                      reads=[bank_b[pg0], small_b], writes=[sg_b[r0]])
                tr.op(ACT, lambda e, r1=r1: e.activation(sg_t[:, r1 * T:(r1 + 1) * T], bankA(pg1), AF.Tanh, scale=0.5,
                                                         bias=gbh_t[:, 8 + m:8 + m + 1]),
                      reads=[bank_b[pg1], small_b], writes=[sg_b[r1]])
                tr.op(DVE, lambda e, r0=r0: e.scalar_tensor_tensor(tmp_t[:, 0:T], sg_t[:, r0 * T:(r0 + 1) * T], 1.0, bankA(pr),
                                                                    ALU.add, ALU.mult),
                      reads=[sg_b[r0], bank_b[pr]], writes=[tmp_b[0]])
                tr.op(DVE, lambda e, r1=r1: e.scalar_tensor_tensor(tmp_t[:, T:2 * T], sg_t[:, r1 * T:(r1 + 1) * T], 1.0, bankA(pp),
                                                                    ALU.add, ALU.mult),
                      reads=[sg_b[r1], bank_b[pp]], writes=[tmp_b[1]])
                tr.op(DVE, lambda e: e.tensor_tensor(ar(A_MERGED, m), tmp_t[:, 0:T], tmp_t[:, T:2 * T], ALU.add),
                      reads=[tmp_b[0], tmp_b[1]], writes=[merged_b[m]])
                release()
            rr["sq"] = 0
            for s in range(2):
                slot = acquire()
                for mm in range(4):
                    m = 4 * s + mm
                    po = nb()
                    pe_group(po, [(bankA(po), slab(slot, kc * 512 + mm * 128, kc * 512 + mm * 128 + 128), ar(A_MERGED, kc), [merged_b[kc]])
                                  for kc in range(KC)], reads=[ring_b[slot]])
                    tr.op(DVE, lambda e, m=m, po=po: e.scalar_tensor_tensor(hA(m), bankA(po), 0.5, hA(m), ALU.mult, ALU.add),
                          reads=[bank_b[po], hb[m]], writes=[hb[m]])
                    norm_step(hb, hA, m)
                release()
            inherit(hid_b, mixer_alias)

        def sq_ones_task(q, kc):
            def t():
                if kc == 0:
                    rr["sq"] = 0
                norm_step(h_b2[q], hAq(q), kc)
            return t

        def prologue_tasks(ti):
            q = ti % 2
            t0 = ti * T

            def t_load():
                tr.dma(SP, lambda e: [e.dma_start(out=h_t[:, q * KC * T:(q + 1) * KC * T].rearrange("p (k t) -> p k t", t=T),
                                                  in_=xT.rearrange("(kc p) s -> p kc s", p=128)[:, :, t0:t0 + T])],
                       h_s2[q], writes=h_b2[q])
                gen_rope_tables(ti)
            return [t_load] + [sq_ones_task(q, kc) for kc in range(KC)]

        def prologue_finish(ti):
            q = ti % 2
            bg_flush()
            norm_finish(1.0 / D, NORMBANK)
            for kc in range(KC):
                tr.op(DVE, lambda e: e.scalar_tensor_tensor(xnA(kc), hAq(q)(kc), vecs_t[:, V_G1 + kc:V_G1 + kc + 1],
                                                            rstd_t[:, :], ALU.mult, ALU.mult),
                      reads=[h_b2[q][kc], rstd_b, const_b], writes=[xn_b[kc]])

        def final_tasks(ti):
            q = ti % 2
            t0 = ti * T
            tasks = [sq_ones_task(q, kc) for kc in range(KC)]
            tasks.append(lambda: norm_finish(1.0 / D, NORMBANK))

            def out_task(kc):
                def t():
                    r = rot("tmp", 2)
                    tr.op(DVE, lambda e: e.scalar_tensor_tensor(tmp_t[:, r * T:(r + 1) * T], hAq(q)(kc),
                                                                vecs_t[:, V_GF + kc:V_GF + kc + 1], rstd_t[:, :],
                                                                ALU.mult, ALU.mult),
                          reads=[h_b2[q][kc], rstd_b, const_b], writes=[tmp_b[r]])
                    tr.dma(SP, lambda e: [e.dma_start(out=outT[kc * 128:(kc + 1) * 128, t0:t0 + T],
                                                      in_=tmp_t[:, r * T:(r + 1) * T])],
                           ob_s[r], reads=[tmp_b[r]])
                return t
            return tasks + [out_task(kc) for kc in range(KC)]

        bg.extend(prologue_tasks(0))
        prologue_finish(0)
        for ti in range(NT):
            hsel[0] = ti % 2
            ffn(post_res=lambda m: norm_step(hB(), hA, m))
            bg_flush()
            mixer(ti)
            rmsnorm_to_xn(V_G2, stats_done=True)
            if ti + 1 < NT:
                bg.extend(prologue_tasks(ti + 1))
                ffn(between=lambda: prologue_finish(ti + 1))
            else:
                ffn()
            bg_flush()
            bg.extend(final_tasks(ti))
        bg_flush()
        for r in range(2):
            SP.h.wait_ge(ob_s[r].sem, ob_s[r].count)
    return nc


_CACHE = {}


def _consts():
    c = np.zeros((128, NCONST), np.float32)
    half = 128
    invf = (10000.0 ** (-np.arange(half, dtype=np.float32) / half)).astype(np.float32)
    c[:, C_INVF] = invf
    c[:, C_TLOC:C_TLOC + T] = np.arange(T, dtype=np.float32)[None, :]
    c[:, C_ID:C_ID + 128] = np.eye(128, dtype=np.float32)
    cs = np.zeros((128, NSET), np.float32)
    p = np.arange(128)
    for delta in range(4):
        cc = np.arange(128)[None, :] + 128 * delta
        ss = p[:, None]
        cs[:, S_DIST + delta * 128:S_DIST + (delta + 1) * 128] = np.abs(cc - ss).astype(np.float32)
    cc = np.arange(128)[None, :]
    ss = p[:, None]
    valid = ((ss // 64) <= (cc // 64)).astype(np.float32)
    cs[:, S_VALID:S_VALID + 128] = valid
    for tc in range(4):
        cs[:, S_KREV + tc] = (T - 1 - (tc * 128 + p)).astype(np.float32)
    for g, w in enumerate((2, 4, 8, 16)):
        cs[:, S_CNT + g * 16:S_CNT + (g + 1) * 16] = np.minimum(np.arange(16) + 1.0, float(w))[None, :]
    return c, cs


def _pvec(v):
    return np.ascontiguousarray(np.asarray(v, np.float32).reshape(-1, 128).T)


def kernel(x, norm_ffn1, ffn1_w_in, ffn1_w_out, norm_mix, w_in, gate_bias, pool_w, pool_scale,
           w_ret_up, w_pool_up, w_out, norm_ffn2, ffn2_w_in, ffn2_w_out, norm_final):
    x = np.asarray(x, np.float32)
    B = x.shape[0]
    if "nc" not in _CACHE:
        _CACHE["nc"] = build_program()
    nc = _CACHE["nc"]
    vecs = np.zeros((128, NVEC), np.float32)
    vecs[:, V_G1:V_G1 + 8] = _pvec(norm_ffn1[0])
    vecs[:, V_GM:V_GM + 8] = _pvec(norm_mix[0])
    vecs[:, V_G2:V_G2 + 8] = _pvec(norm_ffn2[0])
    vecs[:, V_GF:V_GF + 8] = _pvec(norm_final)
    vecs[:, V_GB0:V_GB0 + 8] = _pvec(np.asarray(gate_bias)[0, 0])
    vecs[:, V_GB1:V_GB1 + 8] = _pvec(np.asarray(gate_bias)[0, 1])
    vecs[:, V_PS:V_PS + 8] = _pvec(np.asarray(pool_scale)[0])
    consts, cset = _consts()
    f = lambda a: np.ascontiguousarray(np.asarray(a, np.float32))
    shared = {
        "w1i": f(ffn1_w_in[0]), "w1o": f(ffn1_w_out[0]), "wi": f(w_in[0]), "pw": f(pool_w[0]),
        "wru": f(w_ret_up[0]), "wpu": f(w_pool_up[0]), "wo": f(w_out[0]),
        "w2i": f(ffn2_w_in[0]), "w2o": f(ffn2_w_out[0]), "consts": consts, "cset": cset, "vecs": vecs,
    }
    in_maps = []
    for b in range(B):
        m = dict(shared)
        m["xT"] = np.ascontiguousarray(x[b].T)
        in_maps.append(m)
    res = run_bass_kernel_spmd(nc, in_maps, core_ids=list(range(B)))
    out = np.empty((B, S, D), np.float32)
    for b in range(B):
        out[b] = res.results[b]["outT"].T
    return out
```

```python
import math
import os
import numpy as np
import concourse.bass as bass
import concourse.mybir as mybir
from concourse.bass_utils import run_bass_kernel_spmd

F32 = mybir.dt.float32
BF16 = mybir.dt.bfloat16
AF = mybir.ActivationFunctionType
ALU = mybir.AluOpType

D = 1024
S = 4096 if not os.environ.get('KDBG_NT') else 512 * int(os.environ['KDBG_NT'])
DFF = 2816
T = 512
NT = S // T
KC = 8
HC = DFF // 128
NSLOT = 4
SLOT = 4096
EPS = 1e-6
LG = [math.log(1.0 - 2.0 ** (-5.0 - h)) for h in range(4)]
LN16 = math.log(1.0 / 16.0)
PI = math.pi
HALO = 16
PL = HALO + T

C_INVF = 0
C_TLOC = 1
C_ID = C_TLOC + T
NCONST = C_ID + 128
S_DIST = 0
S_VALID = S_DIST + 4 * 128
S_KREV = S_VALID + 128
S_CNT = S_KREV + 4
NSET = S_CNT + 64
V_G1, V_GM, V_G2, V_GF, V_GB0, V_GB1, V_PS = 0, 8, 16, 24, 32, 40, 48
NVEC = 56


class Buf:
    __slots__ = ("name", "w", "r", "excl")

    def __init__(self, name, excl=False):
        self.name = name
        self.w = {}
        self.r = {}
        self.excl = excl


class Eng:
    def __init__(self, name, handle, sem, self_sync=True):
        self.name = name
        self.h = handle
        self.sem = sem
        self.count = 0
        self.waited = {}
        self.self_sync = self_sync


class DSem:
    def __init__(self, sem):
        self.sem = sem
        self.count = 0


def _merge(d, src):
    for k, v in src.items():
        if d.get(k, (None, 0))[1] < v[1]:
            d[k] = v


class Tracker:
    def deps_for(self, reads, writes):
        deps = {}
        for b in reads:
            _merge(deps, b.w)
            if b.excl:
                _merge(deps, b.r)
        for b in writes:
            _merge(deps, b.w)
            _merge(deps, b.r)
        return deps

    def wait(self, eng, deps):
        for key, (sem, val) in deps.items():
            if key == id(eng.sem) and not eng.self_sync:
                continue
            if eng.waited.get(key, 0) >= val:
                continue
            eng.h.wait_ge(sem, val)
            eng.waited[key] = val

    def record(self, ms_key, ms, reads, writes):
        for b in reads:
            if b.excl:
                b.w = {ms_key: ms}
                b.r = {}
            else:
                if b.r.get(ms_key, (None, 0))[1] < ms[1]:
                    b.r[ms_key] = ms
        for b in writes:
            b.w = {ms_key: ms}
            b.r = {}

    def op(self, eng, fn, reads=(), writes=()):
        self.wait(eng, self.deps_for(reads, writes))
        ins = fn(eng.h)
        eng.count += 1
        ins.then_inc(eng.sem, 1)
        self.record(id(eng.sem), (eng.sem, eng.count), reads, writes)

    def dma(self, eng, fn, dsem, reads=(), writes=()):
        self.wait(eng, self.deps_for(reads, writes))
        inss = fn(eng.h)
        for ins in inss:
            ins.then_inc(dsem.sem, 16)
            dsem.count += 16
        self.record(id(dsem.sem), (dsem.sem, dsem.count), reads, writes)


def inherit(new_bufs, old_bufs):
    for nb_ in new_bufs:
        for ob in old_bufs:
            _merge(nb_.r, ob.w)
            _merge(nb_.r, ob.r)


def build_program():
    nc = bass.Bass("TRN2", target_bir_lowering=False)
    dt = nc.dram_tensor
    xT = dt("xT", [D, S], F32, kind="ExternalInput").ap()
    w1i = dt("w1i", [D, 2 * DFF], F32, kind="ExternalInput").ap()
    w1o = dt("w1o", [DFF, D], F32, kind="ExternalInput").ap()
    wi = dt("wi", [D, 7168], F32, kind="ExternalInput").ap()
    pw = dt("pw", [4, 256, 256], F32, kind="ExternalInput").ap()
    wru = dt("wru", [D, D], F32, kind="ExternalInput").ap()
    wpu = dt("wpu", [D, D], F32, kind="ExternalInput").ap()
    wo = dt("wo", [D, D], F32, kind="ExternalInput").ap()
    w2i = dt("w2i", [D, 2 * DFF], F32, kind="ExternalInput").ap()
    w2o = dt("w2o", [DFF, D], F32, kind="ExternalInput").ap()
    consts_d = dt("consts", [128, NCONST], F32, kind="ExternalInput").ap()
    cset_d = dt("cset", [128, NSET], F32, kind="ExternalInput").ap()
    vecs_d = dt("vecs", [128, NVEC], F32, kind="ExternalInput").ap()
    outT = dt("outT", [D, S], F32, kind="ExternalOutput").ap()

    def kview(w):
        return w.rearrange("(kc p) n -> p kc n", p=128)

    w1i_v, w1o_v, wi_v, wru_v, wpu_v, wo_v, w2i_v, w2o_v = map(kview, (w1i, w1o, wi, wru, wpu, wo, w2i, w2o))

    sb = nc.alloc_sbuf_tensor
    h_t = sb("h", [128, 2 * KC * T], F32)
    xn_t = sb("xn", [128, KC * T], BF16)
    ring_t = sb("ring", [128, NSLOT * SLOT], BF16)
    A_QT = 0
    A_KT = A_QT + KC * T
    A_KTOK = A_KT + KC * T
    A_VTOK = A_KTOK + 4 * 1024
    A_SGR = A_VTOK + 4 * 1024
    A_POOLED = A_SGR + KC * T
    A_POOLOUT = A_POOLED + KC * T
    A_QD = A_POOLOUT + KC * T
    A_SC = A_QD + KC * T
    SC_W = 1280
    A_END = A_SC + 2 * SC_W
    arena_t = sb("arena", [128, A_END], BF16)
    A_HID = 0
    A_MERGED = A_QT
    A_RETN = A_POOLED
    assert HC * T <= A_END
    sq_t = sb("sq", [128, 2 * T], BF16)
    rstd_t = sb("rstd", [128, T], F32)
    std_t = sb("std", [128, T], F32)
    acc_t = std_t
    sg_t = sb("sg", [128, 3 * T], F32)
    tmp_t = sb("tmp", [128, 2 * T], F32)
    pbuf_t = sb("pbuf", [128, 2 * PL], F32)
    pa_t = sb("pwin", [128, 2 * PL], F32)
    halo_t = sb("halo", [128, KC * HALO], F32)
    s32_t = sb("s32", [128, 4 * 512], F32)
    sbf_t = sb("sbf", [128, 4 * 512], BF16)
    cos_t = sb("cos", [128, T], F32)
    sin_t = sb("sin", [128, T], F32)
    ang_t = sb("ang", [128, 3 * T], F32)
    dq_t = sb("dq", [128, 4 * T], F32)
    mask_t = sb("mask", [128, 4 * 512], F32)
    consts_t = sb("constsb", [128, NCONST], F32)
    vecs_t = sb("vecsb", [128, NVEC], F32)
    gbh_t = sb("gbh", [128, 16], F32)
    ksc_t = sb("ksc", [128, 16], F32)
    invc_t = sb("invc", [128, 64], F32)
    ones_t = sb("ones", [128, 128], BF16)
    ident_t = sb("ident", [128, 128], BF16)
    epsc_t = sb("epsc", [128, 4], F32)
    banks_t = [nc.alloc_psum_tensor(f"bank{i}", [128, 512], F32) for i in range(8)]

    tr = Tracker()
    from contextlib import ExitStack

    with ExitStack() as es:
        def sem(name):
            return es.enter_context(nc.semaphore(name))

        PE = Eng("pe", nc.tensor, sem("s_pe"), self_sync=False)
        ACT = Eng("act", nc.scalar, sem("s_act"))
        DVE = Eng("dve", nc.vector, sem("s_dve"))
        POOLQ = Eng("poolq", nc.gpsimd, sem("s_pool"))
        SP = Eng("sp", nc.sync, sem("s_sp"))

        ring_b = [Buf(f"ring{i}") for i in range(NSLOT)]
        ring_s = [DSem(sem(f"s_ring{i}")) for i in range(NSLOT)]
        h_b2 = [[Buf(f"h{q}_{k}") for k in range(KC)] for q in range(2)]
        h_s2 = [DSem(sem("s_hld0")), DSem(sem("s_hld1"))]
        xn_b = [Buf(f"xn{k}") for k in range(KC)]
        hid_b = [Buf(f"hid{j}") for j in range(HC)]
        qT_b = [Buf(f"qT{k}") for k in range(KC)]
        kT_b = [Buf(f"kT{k}") for k in range(KC)]
        ktok_b = [Buf(f"ktok{k}") for k in range(4)]
        vtok_b = [Buf(f"vtok{k}") for k in range(4)]
        sgr_b = [Buf(f"sgr{k}") for k in range(KC)]
        pooled_b = [Buf(f"pooled{k}") for k in range(KC)]
        poolout_b = [Buf(f"poolout{k}") for k in range(KC)]
        merged_b = [Buf(f"merged{k}") for k in range(KC)]
        retn_b = [Buf(f"retn{k}") for k in range(KC)]
        qd_b = [Buf(f"qd{k}") for k in range(KC)]
        sc_b = [Buf(f"sc{r}") for r in range(2)]
        sq_b = [Buf(f"sq{r}") for r in range(2)]
        rstd_b = Buf("rstd")
        std_b = Buf("std")
        acc_b = std_b
        sg_b = [Buf(f"sg{r}") for r in range(3)]
        tmp_b = [Buf(f"tmp{r}") for r in range(2)]
        pbuf_b = [Buf(f"pbuf{r}") for r in range(2)]
        pa_b = [Buf(f"pa{r}") for r in range(2)]
        halo_b = [Buf(f"halo{k}") for k in range(KC)]
        s32_b = [Buf(f"s32{k}") for k in range(4)]
        sbf_b = [Buf(f"sbf{k}") for k in range(4)]
        cos_b = Buf("cos")
        sin_b = Buf("sin")
        ang_b = [Buf("ang0"), Buf("ang1"), Buf("ang2")]
        dq_b = Buf("dq")
        mask_b = Buf("mask")
        ob_s = [DSem(sem("s_ob0")), DSem(sem("s_ob1"))]
        const_b = Buf("consts")
        cset_b = Buf("cset")
        const_s = DSem(sem("s_const"))
        cset_s = DSem(sem("s_cset"))
        small_b = Buf("small")
        bank_b = [Buf(f"bank{i}", excl=True) for i in range(8)]

        mixer_alias = (qT_b + kT_b + ktok_b + vtok_b + sgr_b + pooled_b + poolout_b + merged_b + retn_b
                       + qd_b + sc_b)

        hsel = [0]

        def hA(k, a=0, b=T):
            o = hsel[0] * KC * T
            return h_t[:, o + k * T + a:o + k * T + b]

        def hB():
            return h_b2[hsel[0]]

        def hAq(q):
            def f(k, a=0, b=T):
                o = q * KC * T
                return h_t[:, o + k * T + a:o + k * T + b]
            return f

        NORMBANK = 7
        bg = []

        def bg_step():
            if bg:
                bg.pop(0)()

        def bg_flush():
            while bg:
                bg.pop(0)()

        cset_v = arena_t[:, 0:2 * NSET].bitcast(F32)

        def cs_(a, b):
            return cset_v[:, a:b]

        def xnA(k, a=0, b=T):
            return xn_t[:, k * T + a:k * T + b]

        def ar(off, k, a=0, b=T, stride=T):
            return arena_t[:, off + k * stride + a: off + k * stride + b]

        def hidA(j):
            return ar(A_HID, j)

        def qTA(k, a=0, b=T):
            return ar(A_QT, k, a, b)

        def kTA(k, a=0, b=T):
            return ar(A_KT, k, a, b)

        def ktokA(tc, a, b):
            return ar(A_KTOK, tc, a, b, 1024)

        def vtokA(tc, a, b):
            return ar(A_VTOK, tc, a, b, 1024)

        def bankA(i, a=0, b=512):
            return banks_t[i][:, a:b]

        def bankBF(i, a, b):
            return banks_t[i][:, :].bitcast(BF16)[:, a:b]

        def cst(a, b):
            return consts_t[:, a:b]

        bank_rr = [0]

        def nb():
            i = bank_rr[0]
            bank_rr[0] = (i + 1) % 7
            return i

        rr = {}

        def rot(name, n):
            v = rr.get(name, 0)
            rr[name] = (v + 1) % n
            return v

        slab_loaders = []
        st = {"next_load": 0, "cur": 0}

        def slot_view3(slot, kc, w):
            return ring_t[:, slot * SLOT: slot * SLOT + kc * w].rearrange("p (k w) -> p k w", w=w)

        def ld_ffn_in(wv):
            def mk(s):
                def f(e, slot):
                    v = slot_view3(slot, KC, 512)
                    return [e.dma_start(out=v[:, :, 0:256], in_=wv[:, :, 256 * s:256 * s + 256]),
                            e.dma_start(out=v[:, :, 256:512], in_=wv[:, :, DFF + 256 * s:DFF + 256 * s + 256])]
                return f
            return [mk(s) for s in range(HC // 2)]

        def ld_ffn_out(wv):
            def mk(m):
                def f(e, slot):
                    v = slot_view3(slot, HC, 128)
                    return [e.dma_start(out=v, in_=wv[:, :, 128 * m:128 * m + 128])]
                return f
            return [mk(m) for m in range(KC)]

        def ld_cols(wv, c0):
            def f(e, slot):
                v = slot_view3(slot, KC, 512)
                return [e.dma_start(out=v, in_=wv[:, :, c0:c0 + 512])]
            return f

        def ld_poolw(e, slot):
            out = []
            for g in range(4):
                v = ring_t[:, slot * SLOT + g * 512: slot * SLOT + (g + 1) * 512].rearrange("p (k w) -> p k w", w=256)
                out.append(e.dma_start(out=v, in_=pw[g].rearrange("(kc p) d -> p kc d", p=128)))
            return out

        def ld_merge(m):
            def f(e, slot):
                v = slot_view3(slot, KC, 512)
                return [e.dma_start(out=v[:, :, 0:128], in_=wi_v[:, :, 5120 + 128 * m:5120 + 128 * m + 128]),
                        e.dma_start(out=v[:, :, 128:256], in_=wi_v[:, :, 6144 + 128 * m:6144 + 128 * m + 128]),
                        e.dma_start(out=v[:, :, 256:384], in_=wru_v[:, :, 128 * m:128 * m + 128]),
                        e.dma_start(out=v[:, :, 384:512], in_=wpu_v[:, :, 128 * m:128 * m + 128])]
            return f

        tile_loaders = (ld_ffn_in(w1i_v) + ld_ffn_out(w1o_v) + [ld_cols(wi_v, 512 * s) for s in range(10)]
                        + [ld_poolw] + [ld_merge(m) for m in range(KC)] + [ld_cols(wo_v, 512 * s) for s in range(2)]
                        + ld_ffn_in(w2i_v) + ld_ffn_out(w2o_v))
        NSLAB = len(tile_loaders)
        for _ in range(NT):
            slab_loaders.extend(tile_loaders)

        def ensure_loaded(upto):
            while st["next_load"] <= upto and st["next_load"] < len(slab_loaders):
                i = st["next_load"]
                slot = i % NSLOT
                f = slab_loaders[i]
                tr.dma(POOLQ, lambda e, f=f, slot=slot: f(e, slot), ring_s[slot], writes=[ring_b[slot]])
                st["next_load"] += 1

        def acquire():
            i = st["cur"]
            ensure_loaded(i)
            return i % NSLOT

        def release():
            st["cur"] += 1
            ensure_loaded(st["cur"] + NSLOT - 1)

        def slab(slot, a, b):
            return ring_t[:, slot * SLOT + a: slot * SLOT + b]

        ensure_loaded(NSLOT - 1)
        tr.dma(SP, lambda e: [e.dma_start(out=consts_t[:, :], in_=consts_d),
                              e.dma_start(out=vecs_t[:, :], in_=vecs_d)], const_s, writes=[const_b])
        tr.dma(SP, lambda e: [e.dma_start(out=cset_v, in_=cset_d)], cset_s, writes=[cset_b])
        tr.op(DVE, lambda e: e.memset(ones_t[:, :], 1.0), writes=[small_b])
        tr.op(DVE, lambda e: e.memset(s32_t[:, :], 0.0), writes=s32_b)
        tr.op(DVE, lambda e: e.memset(sbf_t[:, :], 0.0), writes=sbf_b)
        tr.op(DVE, lambda e: e.memset(halo_t[:, :], 0.0), writes=halo_b)
        tr.op(DVE, lambda e: e.memset(epsc_t[:, 0:1], EPS), writes=[small_b])
        tr.op(DVE, lambda e: e.memset(epsc_t[:, 1:2], LN16), reads=[small_b], writes=[small_b])
        tr.op(DVE, lambda e: e.tensor_copy(ident_t[:, :], cst(C_ID, C_ID + 128)), reads=[const_b, small_b], writes=[small_b])
        tr.op(DVE, lambda e: e.tensor_scalar_mul(gbh_t[:, :], vecs_t[:, V_GB0:V_GB0 + 16], 0.5),
              reads=[const_b, small_b], writes=[small_b])
        tr.op(DVE, lambda e: e.reciprocal(invc_t[:, :], cs_(S_CNT, S_CNT + 64)), reads=[cset_b, small_b], writes=[small_b])
        tr.op(DVE, lambda e: e.tensor_scalar_add(ang_t[:, 0:T], cst(C_TLOC, C_TLOC + T), 1.0), reads=[const_b], writes=[ang_b[0]])
        for hh in range(4):
            tr.op(ACT, lambda e, hh=hh: e.activation(dq_t[:, hh * T:(hh + 1) * T], ang_t[:, 0:T], AF.Exp, scale=LG[hh]),
                  reads=[ang_b[0]], writes=[dq_b])
            tr.op(ACT, lambda e, hh=hh: e.activation(mask_t[:, hh * 512:(hh + 1) * 512], cs_(S_DIST, S_DIST + 512), AF.Exp,
                                                     scale=LG[hh], bias=epsc_t[:, 1:2]),
                  reads=[cset_b, small_b], writes=[mask_b])
            tr.op(ACT, lambda e, hh=hh: e.activation(ksc_t[:, hh * 4:(hh + 1) * 4], cs_(S_KREV, S_KREV + 4), AF.Exp,
                                                     scale=LG[hh], bias=epsc_t[:, 1:2]),
                  reads=[cset_b, small_b], writes=[small_b])
        for hh in range(4):
            tr.op(DVE, lambda e, hh=hh: e.tensor_tensor(mask_t[:, hh * 512:hh * 512 + 128], mask_t[:, hh * 512:hh * 512 + 128],
                                                        cs_(S_VALID, S_VALID + 128), ALU.mult),
                  reads=[cset_b, mask_b], writes=[mask_b])
        inherit(hid_b + mixer_alias, [cset_b])

        def pe_group(bank, mms, reads):
            tr.wait(PE, tr.deps_for(reads, [bank_b[bank]]))
            ins = None
            n = len(mms)
            extra = []
            for i, mm in enumerate(mms):
                if len(mm) == 4:
                    tr.wait(PE, tr.deps_for(mm[3], []))
                    extra.extend(mm[3])
                ins = PE.h.matmul(mm[0], mm[1], mm[2], start=(i == 0), stop=(i == n - 1))
            PE.count += 1
            ins.then_inc(PE.sem, 1)
            tr.record(id(PE.sem), (PE.sem, PE.count), list(reads) + extra, [bank_b[bank]])

        def norm_stats(src_b, srcA, nchunks, inv_n, pn=None):
            if pn is None:
                pn = nb()
            for kc in range(nchunks):
                r = rot("sq", 2)
                tr.op(ACT, lambda e, kc=kc, r=r: e.activation(sq_t[:, r * T:(r + 1) * T], srcA(kc), AF.Square),
                      reads=[src_b[kc]], writes=[sq_b[r]])
                tr.op(PE, lambda e, kc=kc, r=r: e.matmul(bankA(pn), ones_t[:, :], sq_t[:, r * T:(r + 1) * T],
                                                         start=(kc == 0), stop=(kc == nchunks - 1)),
                      reads=[sq_b[r], small_b], writes=[bank_b[pn]])
            rstd_from(pn, inv_n)

        def rstd_from(pn, inv_n):
            tr.op(ACT, lambda e: e.activation(std_t[:, :], bankA(pn), AF.Ln, bias=epsc_t[:, 0:1], scale=inv_n),
                  reads=[bank_b[pn], small_b], writes=[std_b])
            tr.op(ACT, lambda e: e.activation(rstd_t[:, :], std_t[:, :], AF.Exp, scale=-0.5), reads=[std_b], writes=[rstd_b])

        def norm_step(src_b, srcA, kc):
            r = rot("sq", 2)
            tr.op(ACT, lambda e: e.activation(sq_t[:, r * T:(r + 1) * T], srcA(kc), AF.Square),
                  reads=[src_b[kc]], writes=[sq_b[r]])
            if kc == 0:
                return
            if kc == 1:
                tr.op(DVE, lambda e: e.tensor_tensor(acc_t[:, :], sq_t[:, 0:T], sq_t[:, T:2 * T], ALU.add),
                      reads=[sq_b[0], sq_b[1]], writes=[acc_b])
            elif kc < KC - 1:
                tr.op(DVE, lambda e: e.tensor_tensor(acc_t[:, :], acc_t[:, :], sq_t[:, r * T:(r + 1) * T], ALU.add),
                      reads=[acc_b, sq_b[r]], writes=[acc_b])
            else:
                assert r == 1
                tr.op(DVE, lambda e: e.tensor_tensor(sq_t[:, 0:T], acc_t[:, :], sq_t[:, T:2 * T], ALU.add),
                      reads=[acc_b, sq_b[1], sq_b[0]], writes=[sq_b[0]])

        def norm_finish(inv_n, pn=None):
            if pn is None:
                pn = nb()
            tr.op(PE, lambda e: e.matmul(bankA(pn), ones_t[:, :], sq_t[:, 0:T], start=True, stop=True),
                  reads=[sq_b[0], small_b], writes=[bank_b[pn]])
            rstd_from(pn, inv_n)

        scr_b = Buf("actscr")

        def preload_ln_table():
            tr.op(ACT, lambda e: e.activation(epsc_t[:, 3:4], epsc_t[:, 0:1], AF.Ln), reads=[small_b], writes=[scr_b])

        def rmsnorm_to_xn(gcol, stats_done=False):
            hb = hB()
            if not stats_done:
                rr["sq"] = 0
                for kc in range(KC):
                    norm_step(hb, hA, kc)
            norm_finish(1.0 / D)
            for kc in range(KC):
                tr.op(DVE, lambda e, kc=kc: e.scalar_tensor_tensor(xnA(kc), hA(kc), vecs_t[:, gcol + kc:gcol + kc + 1],
                                                                   rstd_t[:, :], ALU.mult, ALU.mult),
                      reads=[hb[kc], rstd_b, const_b], writes=[xn_b[kc]])

        def xn_mms(bank, slot, col):
            return [(bankA(bank), slab(slot, kc * 512 + col, kc * 512 + col + 128), xnA(kc), [xn_b[kc]]) for kc in range(KC)]

        def ffn(between=None, post_res=None):
            hb = hB()
            for s in range(HC // 2):
                slot = acquire()
                for jj in range(2):
                    j = 2 * s + jj
                    pg, pu = nb(), nb()
                    pe_group(pg, xn_mms(pg, slot, jj * 128), reads=[ring_b[slot]])
                    pe_group(pu, xn_mms(pu, slot, 256 + jj * 128), reads=[ring_b[slot]])
                    r = rot("sg", 3)
                    tr.op(ACT, lambda e, r=r, pg=pg: e.activation(sg_t[:, r * T:(r + 1) * T], bankA(pg), AF.Silu),
                          reads=[bank_b[pg]], writes=[sg_b[r]])
                    tr.op(DVE, lambda e, r=r, pu=pu, j=j: e.tensor_tensor(hidA(j), bankA(pu), sg_t[:, r * T:(r + 1) * T], ALU.mult),
                          reads=[bank_b[pu], sg_b[r]], writes=[hid_b[j]])
                    bg_step()
                release()
            if between is not None:
                between()
            bg_flush()
            if post_res is not None:
                rr["sq"] = 0
                preload_ln_table()
            for m in range(KC):
                slot = acquire()
                po = nb()
                pe_group(po, [(bankA(po), slab(slot, kc * 128, kc * 128 + 128), hidA(kc), [hid_b[kc]]) for kc in range(HC)],
                         reads=[ring_b[slot]])
                tr.op(DVE, lambda e, m=m, po=po: e.scalar_tensor_tensor(hA(m), bankA(po), 0.5, hA(m), ALU.mult, ALU.add),
                      reads=[bank_b[po], hb[m]], writes=[hb[m]])
                if post_res is not None:
                    post_res(m)
                bg_step()
                release()

        def rotary(pa, pb, dst_b, dstA, c0):
            t0, t1 = 0, 1
            tr.op(DVE, lambda e: e.tensor_tensor(tmp_t[:, 0:T], bankA(pa), cos_t[:, :], ALU.mult),
                  reads=[bank_b[pa], cos_b], writes=[tmp_b[0]])
            tr.op(DVE, lambda e: e.tensor_tensor(tmp_t[:, T:2 * T], bankA(pb), sin_t[:, :], ALU.mult),
                  reads=[bank_b[pb], sin_b], writes=[tmp_b[1]])
            tr.op(DVE, lambda e: e.tensor_tensor(dstA(c0), tmp_t[:, 0:T], tmp_t[:, T:2 * T], ALU.subtract),
                  reads=[tmp_b[0], tmp_b[1]], writes=[dst_b[c0]])
            tr.op(DVE, lambda e: e.tensor_tensor(tmp_t[:, 0:T], bankA(pa), sin_t[:, :], ALU.mult),
                  reads=[bank_b[pa], sin_b], writes=[tmp_b[0]])
            tr.op(DVE, lambda e: e.tensor_tensor(tmp_t[:, T:2 * T], bankA(pb), cos_t[:, :], ALU.mult),
                  reads=[bank_b[pb], cos_b], writes=[tmp_b[1]])
            tr.op(DVE, lambda e: e.tensor_tensor(dstA(c0 + 1), tmp_t[:, 0:T], tmp_t[:, T:2 * T], ALU.add),
                  reads=[tmp_b[0], tmp_b[1]], writes=[dst_b[c0 + 1]])

        def gen_rope_tables(ti):
            off = float(ti * T)
            MAGIC = 12582912.0
            TWO_PI = 2.0 * PI
            PI_LO = 3.1415925

            def A(i):
                return ang_t[:, i * T:(i + 1) * T]
            tr.op(DVE, lambda e: e.tensor_scalar_add(A(0), cst(C_TLOC, C_TLOC + T), off), reads=[const_b], writes=[ang_b[0]])
            tr.op(DVE, lambda e: e.tensor_scalar_mul(A(0), A(0), cst(C_INVF, C_INVF + 1)), reads=[const_b, ang_b[0]], writes=[ang_b[0]])

            def reduce_sin(src_i, out_t, out_b):
                tr.op(DVE, lambda e: e.tensor_scalar(A(1), A(src_i), 1.0 / TWO_PI, MAGIC, ALU.mult, ALU.add),
                      reads=[ang_b[src_i]], writes=[ang_b[1]])
                tr.op(DVE, lambda e: e.tensor_scalar(A(1), A(1), MAGIC, TWO_PI, ALU.subtract, ALU.mult),
                      reads=[ang_b[1]], writes=[ang_b[1]])
                tr.op(DVE, lambda e: e.tensor_tensor(A(2), A(src_i), A(1), ALU.subtract),
                      reads=[ang_b[src_i], ang_b[1]], writes=[ang_b[2]])
                tr.op(DVE, lambda e: e.tensor_scalar(A(2), A(2), PI_LO, -PI_LO, ALU.min, ALU.max),
                      reads=[ang_b[2]], writes=[ang_b[2]])
                tr.op(ACT, lambda e: e.activation(out_t[:, :], A(2), AF.Sin), reads=[ang_b[2]], writes=[out_b])
            reduce_sin(0, sin_t, sin_b)
            tr.op(DVE, lambda e: e.tensor_scalar_add(A(2), A(0), PI / 2), reads=[ang_b[0], ang_b[2]], writes=[ang_b[2]])
            reduce_sin(2, cos_t, cos_b)

        def pooling(ti, m, pp):
            g = m // 2
            w = 2 ** (g + 1)
            r = rot("pbuf", 2)
            base = r * PL

            def pbA(a, b):
                return pbuf_t[:, base + a: base + b]

            tr.op(ACT, lambda e: e.activation(pbA(HALO, PL), bankA(pp), AF.Copy), reads=[bank_b[pp]], writes=[pbuf_b[r]])
            tr.op(ACT, lambda e: e.activation(pbA(0, HALO), halo_t[:, m * HALO:(m + 1) * HALO], AF.Copy),
                  reads=[halo_b[m], pbuf_b[r]], writes=[pbuf_b[r]])
            tr.op(ACT, lambda e: e.activation(halo_t[:, m * HALO:(m + 1) * HALO], pbA(T, PL), AF.Copy),
                  reads=[pbuf_b[r]], writes=[halo_b[m]])

            def paA(q, a, b):
                return pa_t[:, q * PL + a: q * PL + b]

            cur = None
            steps = []
            k = 1
            q = 0
            while k < w:
                steps.append((k, q))
                k *= 2
                q ^= 1
            srcA, src_buf = pbA, pbuf_b[r]
            lo = 0
            for (k, q) in steps:
                lo2 = lo + k
                dA = (lambda q: (lambda a, b: paA(q, a, b)))(q)
                tr.op(DVE, lambda e, srcA=srcA, dA=dA, lo2=lo2, k=k: e.tensor_tensor(dA(lo2, PL), srcA(lo2, PL), srcA(lo2 - k, PL - k), ALU.add),
                      reads=[src_buf], writes=[pa_b[q]])
                srcA, src_buf, lo = dA, pa_b[q], lo2
            tr.op(DVE, lambda e, srcA=srcA: e.scalar_tensor_tensor(ar(A_POOLED, m), srcA(HALO, PL), 1.0 / w, pbA(HALO, PL),
                                                                     ALU.mult, ALU.subtract),
                  reads=[src_buf, pbuf_b[r]], writes=[pooled_b[m]])
            if ti == 0:
                tr.op(DVE, lambda e, srcA=srcA: e.tensor_tensor(tmp_t[:, 0:HALO], srcA(HALO, 2 * HALO), invc_t[:, g * 16:(g + 1) * 16], ALU.mult),
                      reads=[src_buf, small_b], writes=[tmp_b[0]])
                tr.op(DVE, lambda e: e.tensor_tensor(ar(A_POOLED, m, 0, HALO), tmp_t[:, 0:HALO], pbA(HALO, 2 * HALO), ALU.subtract),
                      reads=[tmp_b[0], pbuf_b[r], pooled_b[m]], writes=[pooled_b[m]])

        def mixer(ti):
            hb = hB()
            rmsnorm_to_xn(V_GM, stats_done=True)
            inherit(mixer_alias, hid_b)
            for which, dst_b, dstA in ((0, qT_b, qTA), (1, kT_b, kTA)):
                for s in range(2):
                    slot = acquire()
                    for hh in range(2):
                        head = 2 * s + hh
                        pa, pb = nb(), nb()
                        pe_group(pa, xn_mms(pa, slot, hh * 256), reads=[ring_b[slot]])
                        pe_group(pb, xn_mms(pb, slot, hh * 256 + 128), reads=[ring_b[slot]])
                        rotary(pa, pb, dst_b, dstA, 2 * head)
                    release()
            for c in range(KC):
                tr.op(DVE, lambda e: e.tensor_tensor(arena_t[:, A_QD + c * T: A_QD + (c + 1) * T], qTA(c),
                                                     dq_t[:, (c // 2) * T:(c // 2 + 1) * T], ALU.mult),
                      reads=[qT_b[c], dq_b], writes=[qd_b[c]])
            for s in range(2):
                slot = acquire()
                for tc in range(4):
                    pv = nb()
                    pe_group(pv, [(bankA(pv), xnA(kc, tc * 128, (tc + 1) * 128), slab(slot, kc * 512, (kc + 1) * 512))
                                  for kc in range(KC)], reads=xn_b + [ring_b[slot]])
                    tr.op(ACT, lambda e, tc=tc, s=s, pv=pv: e.activation(vtokA(tc, s * 512, (s + 1) * 512), bankA(pv), AF.Copy),
                          reads=[bank_b[pv]], writes=[vtok_b[tc]])
                release()
            for s in range(2):
                slot = acquire()
                for mm in range(4):
                    m = 4 * s + mm
                    pg = nb()
                    pe_group(pg, xn_mms(pg, slot, mm * 128), reads=[ring_b[slot]])
                    tr.op(ACT, lambda e, m=m, pg=pg: e.activation(ar(A_SGR, m), bankA(pg), AF.Silu),
                          reads=[bank_b[pg]], writes=[sgr_b[m]])
                release()
            for s in range(2):
                slot = acquire()
                for mm in range(4):
                    m = 4 * s + mm
                    pp = nb()
                    pe_group(pp, xn_mms(pp, slot, mm * 128), reads=[ring_b[slot]])
                    pooling(ti, m, pp)
                release()
            slot = acquire()
            for g in range(4):
                for mm in range(2):
                    po = nb()
                    pe_group(po, [(bankA(po), slab(slot, g * 512 + kc * 256 + mm * 128, g * 512 + kc * 256 + mm * 128 + 128),
                                   ar(A_POOLED, 2 * g + kc)) for kc in range(2)],
                             reads=[pooled_b[2 * g], pooled_b[2 * g + 1], ring_b[slot]])
                    c = 2 * g + mm
                    tr.op(ACT, lambda e, c=c, po=po: e.activation(ar(A_POOLOUT, c), bankA(po), AF.Identity,
                                                                  scale=vecs_t[:, V_PS + c:V_PS + c + 1]),
                          reads=[bank_b[po], const_b], writes=[poolout_b[c]])
            release()
            for tc in range(4):
                pt = nb()

                def ftr(e, tc=tc, pt=pt):
                    ins = None
                    for f in range(KC):
                        ins = e.transpose(bankBF(pt, f * 128, (f + 1) * 128), kTA(f, tc * 128, (tc + 1) * 128), ident_t[:, :])
                    return ins
                tr.op(PE, ftr, reads=kT_b + [small_b], writes=[bank_b[pt]])
                for head in range(4):
                    tr.op(ACT, lambda e, tc=tc, pt=pt, head=head: e.activation(
                        ktokA(tc, head * 256, (head + 1) * 256), bankBF(pt, head * 256, (head + 1) * 256), AF.Identity,
                        scale=ksc_t[:, head * 4 + tc:head * 4 + tc + 1]),
                        reads=[bank_b[pt], small_b], writes=[ktok_b[tc]])
            inherit(retn_b, pooled_b)
            offs = [0, 512, 896, 1152]
            hstate = {}

            def scA(r, sbk, n):
                return arena_t[:, A_SC + r * SC_W + offs[sbk]: A_SC + r * SC_W + offs[sbk] + n]

            def qdA(c):
                return arena_t[:, A_QD + c * T: A_QD + (c + 1) * T]

            def stage_S(head):
                r = rot("sc", 2)
                for sbk in range(4):
                    n = (4 - sbk) * 128
                    psc = rot("scbank", 2)
                    pe_group(psc, [(bankA(psc, 0, n), kTA(2 * head + i, sbk * 128, (sbk + 1) * 128), qTA(2 * head + i, sbk * 128, T))
                                   for i in range(2)], reads=[kT_b[2 * head], kT_b[2 * head + 1], qT_b[2 * head], qT_b[2 * head + 1]])
                    tr.op(DVE, lambda e: e.tensor_tensor(scA(r, sbk, n), bankA(psc, 0, n), mask_t[:, head * 512: head * 512 + n], ALU.mult),
                          reads=[bank_b[psc], mask_b], writes=[sc_b[r]])
                hstate[head] = r

            def stage_O(head):
                r = hstate[head]
                pos = [2, 3] if head % 2 == 0 else [4, 5]
                pn = 6
                for ec in range(2):
                    po = pos[ec]
                    mms = []
                    for sbk in range(4):
                        n = (4 - sbk) * 128
                        mms.append((bankA(po, sbk * 128, T), vtokA(sbk, head * 256 + ec * 128, head * 256 + ec * 128 + 128),
                                    scA(r, sbk, n)))
                    for i in range(2):
                        mms.append((bankA(po), sbf_t[:, head * 512 + i * 256 + ec * 128: head * 512 + i * 256 + ec * 128 + 128],
                                    qdA(2 * head + i)))
                    pe_group(po, mms, reads=vtok_b + [sc_b[r], sbf_b[head], qd_b[2 * head], qd_b[2 * head + 1]])
                    rs = rot("sq", 2)
                    tr.op(ACT, lambda e: e.activation(sq_t[:, rs * T:(rs + 1) * T], bankA(po), AF.Square),
                          reads=[bank_b[po]], writes=[sq_b[rs]])
                    tr.op(PE, lambda e: e.matmul(bankA(pn), ones_t[:, :], sq_t[:, rs * T:(rs + 1) * T],
                                                 start=(ec == 0), stop=(ec == 1)),
                          reads=[sq_b[rs], small_b], writes=[bank_b[pn]])

            def stage_F(head):
                pos = [2, 3] if head % 2 == 0 else [4, 5]
                rstd_from(6, 1.0 / 256)
                for ec in range(2):
                    c = 2 * head + ec
                    rt = rot("tmp", 2)
                    tr.op(DVE, lambda e: e.tensor_tensor(tmp_t[:, rt * T:(rt + 1) * T], bankA(pos[ec]), rstd_t[:, :], ALU.mult),
                          reads=[bank_b[pos[ec]], rstd_b], writes=[tmp_b[rt]])
                    tr.op(DVE, lambda e: e.tensor_tensor(ar(A_RETN, c), tmp_t[:, rt * T:(rt + 1) * T], ar(A_SGR, c), ALU.mult),
                          reads=[tmp_b[rt], sgr_b[c]], writes=[retn_b[c]])
                pS = 7
                mms = []
                for i in range(2):
                    for tc in range(4):
                        mms.append((bankA(pS, i * 256, (i + 1) * 256), ktokA(tc, head * 256 + i * 128, head * 256 + i * 128 + 128),
                                    vtokA(tc, head * 256, head * 256 + 256)))

                def fst(e):
                    ins = None
                    for idx, (o, l, rr_) in enumerate(mms):
                        ins = e.matmul(o, l, rr_, start=(idx % 4 == 0), stop=(idx % 4 == 3))
                    return ins
                tr.op(PE, fst, reads=ktok_b + vtok_b, writes=[bank_b[pS]])
                cd = math.exp(LG[head] * T)
                tr.op(DVE, lambda e: e.scalar_tensor_tensor(s32_t[:, head * 512:(head + 1) * 512],
                                                            s32_t[:, head * 512:(head + 1) * 512], cd, bankA(pS),
                                                            ALU.mult, ALU.add),
                      reads=[bank_b[pS], s32_b[head]], writes=[s32_b[head]])
                tr.op(ACT, lambda e: e.activation(sbf_t[:, head * 512:(head + 1) * 512], s32_t[:, head * 512:(head + 1) * 512], AF.Copy),
                      reads=[s32_b[head]], writes=[sbf_b[head]])

            if os.environ.get("KDBG_SEQRET"):
                for hd in range(4):
                    stage_S(hd)
                    stage_O(hd)
                    stage_F(hd)
            else:
                stage_S(0)
                stage_S(1)
                stage_O(0)
                stage_S(2)
                stage_F(0)
                stage_O(1)
                stage_S(3)
                stage_F(1)
                stage_O(2)
                stage_F(2)
                stage_O(3)
                stage_F(3)
            inherit(merged_b, qT_b + kT_b + ktok_b)
            for m in range(KC):
                slot = acquire()
                pg0, pg1, pr, pp = nb(), nb(), nb(), nb()
                pe_group(pg0, xn_mms(pg0, slot, 0), reads=[ring_b[slot]])
                pe_group(pg1, xn_mms(pg1, slot, 128), reads=[ring_b[slot]])
                pe_group(pr, [(bankA(pr), slab(slot, kc * 512 + 256, kc * 512 + 384), ar(A_RETN, kc)) for kc in range(KC)],
                         reads=retn_b + [ring_b[slot]])
                pe_group(pp, [(bankA(pp), slab(slot, kc * 512 + 384, kc * 512 + 512), ar(A_POOLOUT, kc)) for kc in range(KC)],
                         reads=poolout_b + [ring_b[slot]])
                r0, r1 = rot("sg", 3), rot("sg", 3)
                tr.op(ACT, lambda e, r0=r0: e.activation(sg_t[:, r0 * T:(r0 + 1) * T], bankA(pg0), AF.Tanh, scale=0.5,
                                                         bias=gbh_t[:, m:m + 1]),
                      reads=[bank_b[pg0], small_b], writes=[sg_b[r0]])
                tr.op(ACT, lambda e, r1=r1: e.activation(sg_t[:, r1 * T:(r1 + 1) * T], bankA(pg1), AF.Tanh, scale=0.5,
                                                         bias=gbh_t[:, 8 + m:8 + m + 1]),
                      reads=[bank_b[pg1], small_b], writes=[sg_b[r1]])
                tr.op(DVE, lambda e, r0=r0: e.scalar_tensor_tensor(tmp_t[:, 0:T], sg_t[:, r0 * T:(r0 + 1) * T], 1.0, bankA(pr),
                                                                    ALU.add, ALU.mult),
                      reads=[sg_b[r0], bank_b[pr]], writes=[tmp_b[0]])
                tr.op(DVE, lambda e, r1=r1: e.scalar_tensor_tensor(tmp_t[:, T:2 * T], sg_t[:, r1 * T:(r1 + 1) * T], 1.0, bankA(pp),
                                                                    ALU.add, ALU.mult),
                      reads=[sg_b[r1], bank_b[pp]], writes=[tmp_b[1]])
                tr.op(DVE, lambda e: e.tensor_tensor(ar(A_MERGED, m), tmp_t[:, 0:T], tmp_t[:, T:2 * T], ALU.add),
                      reads=[tmp_b[0], tmp_b[1]], writes=[merged_b[m]])
                release()
            rr["sq"] = 0
            preload_ln_table()
            for s in range(2):
                slot = acquire()
                for mm in range(4):
                    m = 4 * s + mm
                    po = nb()
                    pe_group(po, [(bankA(po), slab(slot, kc * 512 + mm * 128, kc * 512 + mm * 128 + 128), ar(A_MERGED, kc), [merged_b[kc]])
                                  for kc in range(KC)], reads=[ring_b[slot]])
                    tr.op(DVE, lambda e, m=m, po=po: e.scalar_tensor_tensor(hA(m), bankA(po), 0.5, hA(m), ALU.mult, ALU.add),
                          reads=[bank_b[po], hb[m]], writes=[hb[m]])
                    norm_step(hb, hA, m)
                release()
            inherit(hid_b, mixer_alias)

        def sq_ones_task(q, kc):
            def t():
                if kc == 0:
                    rr["sq"] = 0
                norm_step(h_b2[q], hAq(q), kc)
            return t

        def prologue_tasks(ti):
            q = ti % 2
            t0 = ti * T

            def t_load():
                tr.dma(SP, lambda e: [e.dma_start(out=h_t[:, q * KC * T:(q + 1) * KC * T].rearrange("p (k t) -> p k t", t=T),
                                                  in_=xT.rearrange("(kc p) s -> p kc s", p=128)[:, :, t0:t0 + T])],
                       h_s2[q], writes=h_b2[q])
                gen_rope_tables(ti)
            return [t_load] + [sq_ones_task(q, kc) for kc in range(KC)]

        def prologue_finish(ti):
            q = ti % 2
            bg_flush()
            norm_finish(1.0 / D, NORMBANK)
            for kc in range(KC):
                tr.op(DVE, lambda e: e.scalar_tensor_tensor(xnA(kc), hAq(q)(kc), vecs_t[:, V_G1 + kc:V_G1 + kc + 1],
                                                            rstd_t[:, :], ALU.mult, ALU.mult),
                      reads=[h_b2[q][kc], rstd_b, const_b], writes=[xn_b[kc]])

        def final_tasks(ti):
            q = ti % 2
            t0 = ti * T
            tasks = [sq_ones_task(q, kc) for kc in range(KC)]
            tasks.append(lambda: norm_finish(1.0 / D, NORMBANK))

            def out_task(kc):
                def t():
                    r = rot("tmp", 2)
                    tr.op(DVE, lambda e: e.scalar_tensor_tensor(tmp_t[:, r * T:(r + 1) * T], hAq(q)(kc),
                                                                vecs_t[:, V_GF + kc:V_GF + kc + 1], rstd_t[:, :],
                                                                ALU.mult, ALU.mult),
                          reads=[h_b2[q][kc], rstd_b, const_b], writes=[tmp_b[r]])
                    tr.dma(SP, lambda e: [e.dma_start(out=outT[kc * 128:(kc + 1) * 128, t0:t0 + T],
                                                      in_=tmp_t[:, r * T:(r + 1) * T])],
                           ob_s[r], reads=[tmp_b[r]])
                return t
            return tasks + [out_task(kc) for kc in range(KC)]

        bg.extend(prologue_tasks(0))
        prologue_finish(0)
        for ti in range(NT):
            hsel[0] = ti % 2
            ffn(post_res=lambda m: norm_step(hB(), hA, m))
            bg_flush()
            mixer(ti)
            rmsnorm_to_xn(V_G2, stats_done=True)
            if ti + 1 < NT:
                bg.extend(prologue_tasks(ti + 1))
                ffn(between=lambda: prologue_finish(ti + 1))
            else:
                ffn()
            bg_flush()
            bg.extend(final_tasks(ti))
        bg_flush()
        for r in range(2):
            SP.h.wait_ge(ob_s[r].sem, ob_s[r].count)
    return nc


_CACHE = {}


def _consts():
    c = np.zeros((128, NCONST), np.float32)
    half = 128
    invf = (10000.0 ** (-np.arange(half, dtype=np.float32) / half)).astype(np.float32)
    c[:, C_INVF] = invf
    c[:, C_TLOC:C_TLOC + T] = np.arange(T, dtype=np.float32)[None, :]
    c[:, C_ID:C_ID + 128] = np.eye(128, dtype=np.float32)
    cs = np.zeros((128, NSET), np.float32)
    p = np.arange(128)
    for delta in range(4):
        cc = np.arange(128)[None, :] + 128 * delta
        ss = p[:, None]
        cs[:, S_DIST + delta * 128:S_DIST + (delta + 1) * 128] = np.abs(cc - ss).astype(np.float32)
    cc = np.arange(128)[None, :]
    ss = p[:, None]
    valid = ((ss // 64) <= (cc // 64)).astype(np.float32)
    cs[:, S_VALID:S_VALID + 128] = valid
    for tc in range(4):
        cs[:, S_KREV + tc] = (T - 1 - (tc * 128 + p)).astype(np.float32)
    for g, w in enumerate((2, 4, 8, 16)):
        cs[:, S_CNT + g * 16:S_CNT + (g + 1) * 16] = np.minimum(np.arange(16) + 1.0, float(w))[None, :]
    return c, cs


def _pvec(v):
    return np.ascontiguousarray(np.asarray(v, np.float32).reshape(-1, 128).T)


def kernel(x, norm_ffn1, ffn1_w_in, ffn1_w_out, norm_mix, w_in, gate_bias, pool_w, pool_scale,
           w_ret_up, w_pool_up, w_out, norm_ffn2, ffn2_w_in, ffn2_w_out, norm_final):
    x = np.asarray(x, np.float32)
    B = x.shape[0]
    if "nc" not in _CACHE:
        _CACHE["nc"] = build_program()
    nc = _CACHE["nc"]
    vecs = np.zeros((128, NVEC), np.float32)
    vecs[:, V_G1:V_G1 + 8] = _pvec(norm_ffn1[0])
    vecs[:, V_GM:V_GM + 8] = _pvec(norm_mix[0])
    vecs[:, V_G2:V_G2 + 8] = _pvec(norm_ffn2[0])
    vecs[:, V_GF:V_GF + 8] = _pvec(norm_final)
    vecs[:, V_GB0:V_GB0 + 8] = _pvec(np.asarray(gate_bias)[0, 0])
    vecs[:, V_GB1:V_GB1 + 8] = _pvec(np.asarray(gate_bias)[0, 1])
    vecs[:, V_PS:V_PS + 8] = _pvec(np.asarray(pool_scale)[0])
    consts, cset = _consts()
    f = lambda a: np.ascontiguousarray(np.asarray(a, np.float32))
    shared = {
        "w1i": f(ffn1_w_in[0]), "w1o": f(ffn1_w_out[0]), "wi": f(w_in[0]), "pw": f(pool_w[0]),
        "wru": f(w_ret_up[0]), "wpu": f(w_pool_up[0]), "wo": f(w_out[0]),
        "w2i": f(ffn2_w_in[0]), "w2o": f(ffn2_w_out[0]), "consts": consts, "cset": cset, "vecs": vecs,
    }
    in_maps = []
    for b in range(B):
        m = dict(shared)
        m["xT"] = np.ascontiguousarray(x[b].T)
        in_maps.append(m)
    res = run_bass_kernel_spmd(nc, in_maps, core_ids=list(range(B)))
    out = np.empty((B, S, D), np.float32)
    for b in range(B):
        out[b] = res.results[b]["outT"].T
    return out
```

```python
import math
import os
import numpy as np
import concourse.bass as bass
import concourse.mybir as mybir
from concourse.bass_utils import run_bass_kernel_spmd

F32 = mybir.dt.float32
BF16 = mybir.dt.bfloat16
AF = mybir.ActivationFunctionType
ALU = mybir.AluOpType

D = 1024
S = 4096 if not os.environ.get('KDBG_NT') else 512 * int(os.environ['KDBG_NT'])
DFF = 2816
T = 512
NT = S // T
KC = 8
HC = DFF // 128
NSLOT = 4
SLOT = 4096
EPS = 1e-6
LG = [math.log(1.0 - 2.0 ** (-5.0 - h)) for h in range(4)]
LN16 = math.log(1.0 / 16.0)
PI = math.pi
HALO = 16
PL = HALO + T

C_INVF = 0
C_TLOC = 1
C_ID = C_TLOC + T
NCONST = C_ID + 128
S_DIST = 0
S_VALID = S_DIST + 4 * 128
S_KREV = S_VALID + 128
S_CNT = S_KREV + 4
NSET = S_CNT + 64
V_G1, V_GM, V_G2, V_GF, V_GB0, V_GB1, V_PS = 0, 8, 16, 24, 32, 40, 48
NVEC = 56


class Buf:
    __slots__ = ("name", "w", "r", "excl")

    def __init__(self, name, excl=False):
        self.name = name
        self.w = {}
        self.r = {}
        self.excl = excl


class Eng:
    def __init__(self, name, handle, sem, self_sync=True):
        self.name = name
        self.h = handle
        self.sem = sem
        self.count = 0
        self.waited = {}
        self.self_sync = self_sync


class DSem:
    def __init__(self, sem):
        self.sem = sem
        self.count = 0


def _merge(d, src):
    for k, v in src.items():
        if d.get(k, (None, 0))[1] < v[1]:
            d[k] = v


class Tracker:
    def deps_for(self, reads, writes):
        deps = {}
        for b in reads:
            _merge(deps, b.w)
            if b.excl:
                _merge(deps, b.r)
        for b in writes:
            _merge(deps, b.w)
            _merge(deps, b.r)
        return deps

    def wait(self, eng, deps):
        for key, (sem, val) in deps.items():
            if key == id(eng.sem) and not eng.self_sync:
                continue
            if eng.waited.get(key, 0) >= val:
                continue
            eng.h.wait_ge(sem, val)
            eng.waited[key] = val

    def record(self, ms_key, ms, reads, writes):
        for b in reads:
            if b.excl:
                b.w = {ms_key: ms}
                b.r = {}
            else:
                if b.r.get(ms_key, (None, 0))[1] < ms[1]:
                    b.r[ms_key] = ms
        for b in writes:
            b.w = {ms_key: ms}
            b.r = {}

    def op(self, eng, fn, reads=(), writes=()):
        self.wait(eng, self.deps_for(reads, writes))
        ins = fn(eng.h)
        eng.count += 1
        ins.then_inc(eng.sem, 1)
        self.record(id(eng.sem), (eng.sem, eng.count), reads, writes)

    def dma(self, eng, fn, dsem, reads=(), writes=()):
        self.wait(eng, self.deps_for(reads, writes))
        inss = fn(eng.h)
        for ins in inss:
            ins.then_inc(dsem.sem, 16)
            dsem.count += 16
        self.record(id(dsem.sem), (dsem.sem, dsem.count), reads, writes)


def inherit(new_bufs, old_bufs):
    for nb_ in new_bufs:
        for ob in old_bufs:
            _merge(nb_.r, ob.w)
            _merge(nb_.r, ob.r)


def build_program():
    nc = bass.Bass("TRN2", target_bir_lowering=False)
    dt = nc.dram_tensor
    xT = dt("xT", [D, S], F32, kind="ExternalInput").ap()
    w1i = dt("w1i", [D, 2 * DFF], F32, kind="ExternalInput").ap()
    w1o = dt("w1o", [DFF, D], F32, kind="ExternalInput").ap()
    wi = dt("wi", [D, 7168], F32, kind="ExternalInput").ap()
    pw = dt("pw", [4, 256, 256], F32, kind="ExternalInput").ap()
    wru = dt("wru", [D, D], F32, kind="ExternalInput").ap()
    wpu = dt("wpu", [D, D], F32, kind="ExternalInput").ap()
    wo = dt("wo", [D, D], F32, kind="ExternalInput").ap()
    w2i = dt("w2i", [D, 2 * DFF], F32, kind="ExternalInput").ap()
    w2o = dt("w2o", [DFF, D], F32, kind="ExternalInput").ap()
    consts_d = dt("consts", [128, NCONST], F32, kind="ExternalInput").ap()
    cset_d = dt("cset", [128, NSET], F32, kind="ExternalInput").ap()
    vecs_d = dt("vecs", [128, NVEC], F32, kind="ExternalInput").ap()
    outT = dt("outT", [D, S], F32, kind="ExternalOutput").ap()

    def kview(w):
        return w.rearrange("(kc p) n -> p kc n", p=128)

    w1i_v, w1o_v, wi_v, wru_v, wpu_v, wo_v, w2i_v, w2o_v = map(kview, (w1i, w1o, wi, wru, wpu, wo, w2i, w2o))

    sb = nc.alloc_sbuf_tensor
    h_t = sb("h", [128, 2 * KC * T], F32)
    xn_t = sb("xn", [128, KC * T], BF16)
    ring_t = sb("ring", [128, NSLOT * SLOT], BF16)
    A_QT = 0
    A_KT = A_QT + KC * T
    A_KTOK = A_KT + KC * T
    A_VTOK = A_KTOK + 4 * 1024
    A_SGR = A_VTOK + 4 * 1024
    A_POOLED = A_SGR + KC * T
    A_POOLOUT = A_POOLED + KC * T
    A_QD = A_POOLOUT + KC * T
    A_SC = A_QD + KC * T
    SC_W = 1280
    A_END = A_SC + 2 * SC_W
    arena_t = sb("arena", [128, A_END], BF16)
    A_HID = 0
    A_MERGED = A_QT
    A_RETN = A_POOLED
    assert HC * T <= A_END
    sq_t = sb("sq", [128, 2 * T], BF16)
    rstd_t = sb("rstd", [128, T], F32)
    std_t = sb("std", [128, T], F32)
    acc_t = std_t
    sg_t = sb("sg", [128, 3 * T], F32)
    tmp_t = sb("tmp", [128, 2 * T], F32)
    pbuf_t = sb("pbuf", [128, 2 * PL], F32)
    pa_t = sb("pwin", [128, 2 * PL], F32)
    halo_t = sb("halo", [128, KC * HALO], F32)
    s32_t = sb("s32", [128, 4 * 512], F32)
    sbf_t = sb("sbf", [128, 4 * 512], BF16)
    cos_t = sb("cos", [128, T], F32)
    sin_t = sb("sin", [128, T], F32)
    ang_t = sb("ang", [128, 3 * T], F32)
    dq_t = sb("dq", [128, 4 * T], F32)
    mask_t = sb("mask", [128, 4 * 512], F32)
    consts_t = sb("constsb", [128, NCONST], F32)
    vecs_t = sb("vecsb", [128, NVEC], F32)
    gbh_t = sb("gbh", [128, 16], F32)
    ksc_t = sb("ksc", [128, 16], F32)
    invc_t = sb("invc", [128, 64], F32)
    ones_t = sb("ones", [128, 128], BF16)
    ident_t = sb("ident", [128, 128], BF16)
    epsc_t = sb("epsc", [128, 4], F32)
    banks_t = [nc.alloc_psum_tensor(f"bank{i}", [128, 512], F32) for i in range(8)]

    tr = Tracker()
    from contextlib import ExitStack

    with ExitStack() as es:
        def sem(name):
            return es.enter_context(nc.semaphore(name))

        PE = Eng("pe", nc.tensor, sem("s_pe"), self_sync=False)
        ACT = Eng("act", nc.scalar, sem("s_act"))
        DVE = Eng("dve", nc.vector, sem("s_dve"))
        POOLQ = Eng("poolq", nc.gpsimd, sem("s_pool"))
        SP = Eng("sp", nc.sync, sem("s_sp"))

        ring_b = [Buf(f"ring{i}") for i in range(NSLOT)]
        ring_s = [DSem(sem(f"s_ring{i}")) for i in range(NSLOT)]
        h_b2 = [[Buf(f"h{q}_{k}") for k in range(KC)] for q in range(2)]
        h_s2 = [DSem(sem("s_hld0")), DSem(sem("s_hld1"))]
        xn_b = [Buf(f"xn{k}") for k in range(KC)]
        hid_b = [Buf(f"hid{j}") for j in range(HC)]
        qT_b = [Buf(f"qT{k}") for k in range(KC)]
        kT_b = [Buf(f"kT{k}") for k in range(KC)]
        ktok_b = [Buf(f"ktok{k}") for k in range(4)]
        vtok_b = [Buf(f"vtok{k}") for k in range(4)]
        sgr_b = [Buf(f"sgr{k}") for k in range(KC)]
        pooled_b = [Buf(f"pooled{k}") for k in range(KC)]
        poolout_b = [Buf(f"poolout{k}") for k in range(KC)]
        merged_b = [Buf(f"merged{k}") for k in range(KC)]
        retn_b = [Buf(f"retn{k}") for k in range(KC)]
        qd_b = [Buf(f"qd{k}") for k in range(KC)]
        sc_b = [Buf(f"sc{r}") for r in range(2)]
        sq_b = [Buf(f"sq{r}") for r in range(2)]
        rstd_b = Buf("rstd")
        std_b = Buf("std")
        acc_b = std_b
        sg_b = [Buf(f"sg{r}") for r in range(3)]
        tmp_b = [Buf(f"tmp{r}") for r in range(2)]
        pbuf_b = [Buf(f"pbuf{r}") for r in range(2)]
        pa_b = [Buf(f"pa{r}") for r in range(2)]
        halo_b = [Buf(f"halo{k}") for k in range(KC)]
        s32_b = [Buf(f"s32{k}") for k in range(4)]
        sbf_b = [Buf(f"sbf{k}") for k in range(4)]
        cos_b = Buf("cos")
        sin_b = Buf("sin")
        ang_b = [Buf("ang0"), Buf("ang1"), Buf("ang2")]
        dq_b = Buf("dq")
        mask_b = Buf("mask")
        ob_s = [DSem(sem("s_ob0")), DSem(sem("s_ob1"))]
        const_b = Buf("consts")
        cset_b = Buf("cset")
        const_s = DSem(sem("s_const"))
        cset_s = DSem(sem("s_cset"))
        small_b = Buf("small")
        bank_b = [Buf(f"bank{i}", excl=True) for i in range(8)]

        mixer_alias = (qT_b + kT_b + ktok_b + vtok_b + sgr_b + pooled_b + poolout_b + merged_b + retn_b
                       + qd_b + sc_b)

        hsel = [0]

        def hA(k, a=0, b=T):
            o = hsel[0] * KC * T
            return h_t[:, o + k * T + a:o + k * T + b]

        def hB():
            return h_b2[hsel[0]]

        def hAq(q):
            def f(k, a=0, b=T):
                o = q * KC * T
                return h_t[:, o + k * T + a:o + k * T + b]
            return f

        NORMBANK = 7
        bg = []

        def bg_step():
            if bg:
                bg.pop(0)()

        def bg_flush():
            while bg:
                bg.pop(0)()

        cset_v = arena_t[:, 0:2 * NSET].bitcast(F32)

        def cs_(a, b):
            return cset_v[:, a:b]

        def xnA(k, a=0, b=T):
            return xn_t[:, k * T + a:k * T + b]

        def ar(off, k, a=0, b=T, stride=T):
            return arena_t[:, off + k * stride + a: off + k * stride + b]

        def hidA(j):
            return ar(A_HID, j)

        def qTA(k, a=0, b=T):
            return ar(A_QT, k, a, b)

        def kTA(k, a=0, b=T):
            return ar(A_KT, k, a, b)

        def ktokA(tc, a, b):
            return ar(A_KTOK, tc, a, b, 1024)

        def vtokA(tc, a, b):
            return ar(A_VTOK, tc, a, b, 1024)

        def bankA(i, a=0, b=512):
            return banks_t[i][:, a:b]

        def bankBF(i, a, b):
            return banks_t[i][:, :].bitcast(BF16)[:, a:b]

        def cst(a, b):
            return consts_t[:, a:b]

        bank_rr = [0]

        def nb():
            i = bank_rr[0]
            bank_rr[0] = (i + 1) % 7
            return i

        rr = {}

        def rot(name, n):
            v = rr.get(name, 0)
            rr[name] = (v + 1) % n
            return v

        slab_loaders = []
        st = {"next_load": 0, "cur": 0}

        def slot_view3(slot, kc, w):
            return ring_t[:, slot * SLOT: slot * SLOT + kc * w].rearrange("p (k w) -> p k w", w=w)

        def ld_ffn_in(wv):
            def mk(s):
                def f(e, slot):
                    v = slot_view3(slot, KC, 512)
                    return [e.dma_start(out=v[:, :, 0:256], in_=wv[:, :, 256 * s:256 * s + 256]),
                            e.dma_start(out=v[:, :, 256:512], in_=wv[:, :, DFF + 256 * s:DFF + 256 * s + 256])]
                return f
            return [mk(s) for s in range(HC // 2)]

        def ld_ffn_out(wv):
            def mk(m):
                def f(e, slot):
                    v = slot_view3(slot, HC, 128)
                    return [e.dma_start(out=v, in_=wv[:, :, 128 * m:128 * m + 128])]
                return f
            return [mk(m) for m in range(KC)]

        def ld_cols(wv, c0):
            def f(e, slot):
                v = slot_view3(slot, KC, 512)
                return [e.dma_start(out=v, in_=wv[:, :, c0:c0 + 512])]
            return f

        def ld_poolw(e, slot):
            out = []
            for g in range(4):
                v = ring_t[:, slot * SLOT + g * 512: slot * SLOT + (g + 1) * 512].rearrange("p (k w) -> p k w", w=256)
                out.append(e.dma_start(out=v, in_=pw[g].rearrange("(kc p) d -> p kc d", p=128)))
            return out

        def ld_merge(m):
            def f(e, slot):
                v = slot_view3(slot, KC, 512)
                return [e.dma_start(out=v[:, :, 0:128], in_=wi_v[:, :, 5120 + 128 * m:5120 + 128 * m + 128]),
                        e.dma_start(out=v[:, :, 128:256], in_=wi_v[:, :, 6144 + 128 * m:6144 + 128 * m + 128]),
                        e.dma_start(out=v[:, :, 256:384], in_=wru_v[:, :, 128 * m:128 * m + 128]),
                        e.dma_start(out=v[:, :, 384:512], in_=wpu_v[:, :, 128 * m:128 * m + 128])]
            return f

        tile_loaders = (ld_ffn_in(w1i_v) + ld_ffn_out(w1o_v) + [ld_cols(wi_v, 512 * s) for s in range(10)]
                        + [ld_poolw] + [ld_merge(m) for m in range(KC)] + [ld_cols(wo_v, 512 * s) for s in range(2)]
                        + ld_ffn_in(w2i_v) + ld_ffn_out(w2o_v))
        NSLAB = len(tile_loaders)
        for _ in range(NT):
            slab_loaders.extend(tile_loaders)

        def ensure_loaded(upto):
            while st["next_load"] <= upto and st["next_load"] < len(slab_loaders):
                i = st["next_load"]
                slot = i % NSLOT
                f = slab_loaders[i]
                tr.dma(POOLQ, lambda e, f=f, slot=slot: f(e, slot), ring_s[slot], writes=[ring_b[slot]])
                st["next_load"] += 1

        def acquire():
            i = st["cur"]
            ensure_loaded(i)
            return i % NSLOT

        def release():
            st["cur"] += 1
            ensure_loaded(st["cur"] + NSLOT - 1)

        def slab(slot, a, b):
            return ring_t[:, slot * SLOT + a: slot * SLOT + b]

        ensure_loaded(NSLOT - 1)
        tr.dma(SP, lambda e: [e.dma_start(out=consts_t[:, :], in_=consts_d),
                              e.dma_start(out=vecs_t[:, :], in_=vecs_d)], const_s, writes=[const_b])
        tr.dma(SP, lambda e: [e.dma_start(out=cset_v, in_=cset_d)], cset_s, writes=[cset_b])
        tr.op(DVE, lambda e: e.memset(ones_t[:, :], 1.0), writes=[small_b])
        tr.op(DVE, lambda e: e.memset(s32_t[:, :], 0.0), writes=s32_b)
        tr.op(DVE, lambda e: e.memset(sbf_t[:, :], 0.0), writes=sbf_b)
        tr.op(DVE, lambda e: e.memset(halo_t[:, :], 0.0), writes=halo_b)
        tr.op(DVE, lambda e: e.memset(epsc_t[:, 0:1], EPS), writes=[small_b])
        tr.op(DVE, lambda e: e.memset(epsc_t[:, 1:2], LN16), reads=[small_b], writes=[small_b])
        tr.op(DVE, lambda e: e.tensor_copy(ident_t[:, :], cst(C_ID, C_ID + 128)), reads=[const_b, small_b], writes=[small_b])
        tr.op(DVE, lambda e: e.tensor_scalar_mul(gbh_t[:, :], vecs_t[:, V_GB0:V_GB0 + 16], 0.5),
              reads=[const_b, small_b], writes=[small_b])
        tr.op(DVE, lambda e: e.reciprocal(invc_t[:, :], cs_(S_CNT, S_CNT + 64)), reads=[cset_b, small_b], writes=[small_b])
        tr.op(DVE, lambda e: e.tensor_scalar_add(ang_t[:, 0:T], cst(C_TLOC, C_TLOC + T), 1.0), reads=[const_b], writes=[ang_b[0]])
        for hh in range(4):
            tr.op(ACT, lambda e, hh=hh: e.activation(dq_t[:, hh * T:(hh + 1) * T], ang_t[:, 0:T], AF.Exp, scale=LG[hh]),
                  reads=[ang_b[0]], writes=[dq_b])
            tr.op(ACT, lambda e, hh=hh: e.activation(mask_t[:, hh * 512:(hh + 1) * 512], cs_(S_DIST, S_DIST + 512), AF.Exp,
                                                     scale=LG[hh], bias=epsc_t[:, 1:2]),
                  reads=[cset_b, small_b], writes=[mask_b])
            tr.op(ACT, lambda e, hh=hh: e.activation(ksc_t[:, hh * 4:(hh + 1) * 4], cs_(S_KREV, S_KREV + 4), AF.Exp,
                                                     scale=LG[hh], bias=epsc_t[:, 1:2]),
                  reads=[cset_b, small_b], writes=[small_b])
        for hh in range(4):
            tr.op(DVE, lambda e, hh=hh: e.tensor_tensor(mask_t[:, hh * 512:hh * 512 + 128], mask_t[:, hh * 512:hh * 512 + 128],
                                                        cs_(S_VALID, S_VALID + 128), ALU.mult),
                  reads=[cset_b, mask_b], writes=[mask_b])
        inherit(hid_b + mixer_alias, [cset_b])

        def pe_group(bank, mms, reads):
            tr.wait(PE, tr.deps_for(reads, [bank_b[bank]]))
            ins = None
            n = len(mms)
            extra = []
            for i, mm in enumerate(mms):
                if len(mm) == 4:
                    tr.wait(PE, tr.deps_for(mm[3], []))
                    extra.extend(mm[3])
                ins = PE.h.matmul(mm[0], mm[1], mm[2], start=(i == 0), stop=(i == n - 1))
            PE.count += 1
            ins.then_inc(PE.sem, 1)
            tr.record(id(PE.sem), (PE.sem, PE.count), list(reads) + extra, [bank_b[bank]])

        def norm_stats(src_b, srcA, nchunks, inv_n, pn=None):
            if pn is None:
                pn = nb()
            for kc in range(nchunks):
                r = rot("sq", 2)
                tr.op(ACT, lambda e, kc=kc, r=r: e.activation(sq_t[:, r * T:(r + 1) * T], srcA(kc), AF.Square),
                      reads=[src_b[kc]], writes=[sq_b[r]])
                tr.op(PE, lambda e, kc=kc, r=r: e.matmul(bankA(pn), ones_t[:, :], sq_t[:, r * T:(r + 1) * T],
                                                         start=(kc == 0), stop=(kc == nchunks - 1)),
                      reads=[sq_b[r], small_b], writes=[bank_b[pn]])
            rstd_from(pn, inv_n)

        def rstd_from(pn, inv_n):
            tr.op(ACT, lambda e: e.activation(std_t[:, :], bankA(pn), AF.Ln, bias=epsc_t[:, 0:1], scale=inv_n),
                  reads=[bank_b[pn], small_b], writes=[std_b])
            tr.op(ACT, lambda e: e.activation(rstd_t[:, :], std_t[:, :], AF.Exp, scale=-0.5), reads=[std_b], writes=[rstd_b])

        def norm_step(src_b, srcA, kc):
            r = rot("sq", 2)
            tr.op(ACT, lambda e: e.activation(sq_t[:, r * T:(r + 1) * T], srcA(kc), AF.Square),
                  reads=[src_b[kc]], writes=[sq_b[r]])
            if kc == 0:
                return
            if kc == 1:
                tr.op(DVE, lambda e: e.tensor_tensor(acc_t[:, :], sq_t[:, 0:T], sq_t[:, T:2 * T], ALU.add),
                      reads=[sq_b[0], sq_b[1]], writes=[acc_b])
            elif kc < KC - 1:
                tr.op(DVE, lambda e: e.tensor_tensor(acc_t[:, :], acc_t[:, :], sq_t[:, r * T:(r + 1) * T], ALU.add),
                      reads=[acc_b, sq_b[r]], writes=[acc_b])
            else:
                assert r == 1
                tr.op(DVE, lambda e: e.tensor_tensor(sq_t[:, 0:T], acc_t[:, :], sq_t[:, T:2 * T], ALU.add),
                      reads=[acc_b, sq_b[1], sq_b[0]], writes=[sq_b[0]])

        def norm_finish(inv_n, pn=None):
            if pn is None:
                pn = nb()
            tr.op(PE, lambda e: e.matmul(bankA(pn), ones_t[:, :], sq_t[:, 0:T], start=True, stop=True),
                  reads=[sq_b[0], small_b], writes=[bank_b[pn]])
            rstd_from(pn, inv_n)

        def rmsnorm_to_xn(gcol, stats_done=False):
            hb = hB()
            if not stats_done:
                rr["sq"] = 0
                for kc in range(KC):
                    norm_step(hb, hA, kc)
            norm_finish(1.0 / D)
            for kc in range(KC):
                tr.op(DVE, lambda e, kc=kc: e.scalar_tensor_tensor(xnA(kc), hA(kc), vecs_t[:, gcol + kc:gcol + kc + 1],
                                                                   rstd_t[:, :], ALU.mult, ALU.mult),
                      reads=[hb[kc], rstd_b, const_b], writes=[xn_b[kc]])

        def xn_mms(bank, slot, col):
            return [(bankA(bank), slab(slot, kc * 512 + col, kc * 512 + col + 128), xnA(kc), [xn_b[kc]]) for kc in range(KC)]

        def pe_multi(banks, cols, slot):
            tr.wait(PE, tr.deps_for([ring_b[slot]], [bank_b[b] for b in banks]))
            ins = None
            for kc in range(KC):
                tr.wait(PE, tr.deps_for([xn_b[kc]], []))
                for b, col in zip(banks, cols):
                    ins = PE.h.matmul(bankA(b), slab(slot, kc * 512 + col, kc * 512 + col + 128), xnA(kc),
                                      start=(kc == 0), stop=(kc == KC - 1))
            PE.count += 1
            ins.then_inc(PE.sem, 1)
            tr.record(id(PE.sem), (PE.sem, PE.count), [ring_b[slot]] + xn_b, [bank_b[b] for b in banks])

        def ffn(between=None, post_res=None, paced=False):
            hb = hB()
            for s in range(HC // 2):
                slot = acquire()
                multi = None
                if paced and s == 0:
                    multi = [nb(), nb(), nb(), nb()]
                    pe_multi(multi, [0, 256, 128, 384], slot)
                for jj in range(2):
                    j = 2 * s + jj
                    if multi is not None:
                        pg, pu = multi[2 * jj], multi[2 * jj + 1]
                    else:
                        pg, pu = nb(), nb()
                        pe_group(pg, xn_mms(pg, slot, jj * 128), reads=[ring_b[slot]])
                        pe_group(pu, xn_mms(pu, slot, 256 + jj * 128), reads=[ring_b[slot]])
                    r = rot("sg", 3)
                    tr.op(ACT, lambda e, r=r, pg=pg: e.activation(sg_t[:, r * T:(r + 1) * T], bankA(pg), AF.Silu),
                          reads=[bank_b[pg]], writes=[sg_b[r]])
                    tr.op(DVE, lambda e, r=r, pu=pu, j=j: e.tensor_tensor(hidA(j), bankA(pu), sg_t[:, r * T:(r + 1) * T], ALU.mult),
                          reads=[bank_b[pu], sg_b[r]], writes=[hid_b[j]])
                    bg_step()
                release()
            if between is not None:
                between()
            bg_flush()
            if post_res is not None:
                rr["sq"] = 0
            for m in range(KC):
                slot = acquire()
                po = nb()
                pe_group(po, [(bankA(po), slab(slot, kc * 128, kc * 128 + 128), hidA(kc), [hid_b[kc]]) for kc in range(HC)],
                         reads=[ring_b[slot]])
                tr.op(DVE, lambda e, m=m, po=po: e.scalar_tensor_tensor(hA(m), bankA(po), 0.5, hA(m), ALU.mult, ALU.add),
                      reads=[bank_b[po], hb[m]], writes=[hb[m]])
                if post_res is not None:
                    post_res(m)
                bg_step()
                release()

        def rotary(pa, pb, dst_b, dstA, c0):
            t0, t1 = 0, 1
            tr.op(DVE, lambda e: e.tensor_tensor(tmp_t[:, 0:T], bankA(pa), cos_t[:, :], ALU.mult),
                  reads=[bank_b[pa], cos_b], writes=[tmp_b[0]])
            tr.op(DVE, lambda e: e.tensor_tensor(tmp_t[:, T:2 * T], bankA(pb), sin_t[:, :], ALU.mult),
                  reads=[bank_b[pb], sin_b], writes=[tmp_b[1]])
            tr.op(DVE, lambda e: e.tensor_tensor(dstA(c0), tmp_t[:, 0:T], tmp_t[:, T:2 * T], ALU.subtract),
                  reads=[tmp_b[0], tmp_b[1]], writes=[dst_b[c0]])
            tr.op(DVE, lambda e: e.tensor_tensor(tmp_t[:, 0:T], bankA(pa), sin_t[:, :], ALU.mult),
                  reads=[bank_b[pa], sin_b], writes=[tmp_b[0]])
            tr.op(DVE, lambda e: e.tensor_tensor(tmp_t[:, T:2 * T], bankA(pb), cos_t[:, :], ALU.mult),
                  reads=[bank_b[pb], cos_b], writes=[tmp_b[1]])
            tr.op(DVE, lambda e: e.tensor_tensor(dstA(c0 + 1), tmp_t[:, 0:T], tmp_t[:, T:2 * T], ALU.add),
                  reads=[tmp_b[0], tmp_b[1]], writes=[dst_b[c0 + 1]])

        def gen_rope_tables(ti):
            off = float(ti * T)
            MAGIC = 12582912.0
            TWO_PI = 2.0 * PI
            PI_LO = 3.1415925

            def A(i):
                return ang_t[:, i * T:(i + 1) * T]
            tr.op(DVE, lambda e: e.tensor_scalar_add(A(0), cst(C_TLOC, C_TLOC + T), off), reads=[const_b], writes=[ang_b[0]])
            tr.op(DVE, lambda e: e.tensor_scalar_mul(A(0), A(0), cst(C_INVF, C_INVF + 1)), reads=[const_b, ang_b[0]], writes=[ang_b[0]])

            def reduce_sin(src_i, out_t, out_b):
                tr.op(DVE, lambda e: e.tensor_scalar(A(1), A(src_i), 1.0 / TWO_PI, MAGIC, ALU.mult, ALU.add),
                      reads=[ang_b[src_i]], writes=[ang_b[1]])
                tr.op(DVE, lambda e: e.tensor_scalar(A(1), A(1), MAGIC, TWO_PI, ALU.subtract, ALU.mult),
                      reads=[ang_b[1]], writes=[ang_b[1]])
                tr.op(DVE, lambda e: e.tensor_tensor(A(2), A(src_i), A(1), ALU.subtract),
                      reads=[ang_b[src_i], ang_b[1]], writes=[ang_b[2]])
                tr.op(DVE, lambda e: e.tensor_scalar(A(2), A(2), PI_LO, -PI_LO, ALU.min, ALU.max),
                      reads=[ang_b[2]], writes=[ang_b[2]])
                tr.op(ACT, lambda e: e.activation(out_t[:, :], A(2), AF.Sin), reads=[ang_b[2]], writes=[out_b])
            reduce_sin(0, sin_t, sin_b)
            tr.op(DVE, lambda e: e.tensor_scalar_add(A(2), A(0), PI / 2), reads=[ang_b[0], ang_b[2]], writes=[ang_b[2]])
            reduce_sin(2, cos_t, cos_b)

        def pooling(ti, m, pp):
            g = m // 2
            w = 2 ** (g + 1)
            r = rot("pbuf", 2)
            base = r * PL

            def pbA(a, b):
                return pbuf_t[:, base + a: base + b]

            tr.op(ACT, lambda e: e.activation(pbA(HALO, PL), bankA(pp), AF.Copy), reads=[bank_b[pp]], writes=[pbuf_b[r]])
            tr.op(ACT, lambda e: e.activation(pbA(0, HALO), halo_t[:, m * HALO:(m + 1) * HALO], AF.Copy),
                  reads=[halo_b[m], pbuf_b[r]], writes=[pbuf_b[r]])
            tr.op(ACT, lambda e: e.activation(halo_t[:, m * HALO:(m + 1) * HALO], pbA(T, PL), AF.Copy),
                  reads=[pbuf_b[r]], writes=[halo_b[m]])

            def paA(q, a, b):
                return pa_t[:, q * PL + a: q * PL + b]

            cur = None
            steps = []
            k = 1
            q = 0
            while k < w:
                steps.append((k, q))
                k *= 2
                q ^= 1
            srcA, src_buf = pbA, pbuf_b[r]
            lo = 0
            for (k, q) in steps:
                lo2 = lo + k
                dA = (lambda q: (lambda a, b: paA(q, a, b)))(q)
                tr.op(DVE, lambda e, srcA=srcA, dA=dA, lo2=lo2, k=k: e.tensor_tensor(dA(lo2, PL), srcA(lo2, PL), srcA(lo2 - k, PL - k), ALU.add),
                      reads=[src_buf], writes=[pa_b[q]])
                srcA, src_buf, lo = dA, pa_b[q], lo2
            tr.op(DVE, lambda e, srcA=srcA: e.scalar_tensor_tensor(ar(A_POOLED, m), srcA(HALO, PL), 1.0 / w, pbA(HALO, PL),
                                                                     ALU.mult, ALU.subtract),
                  reads=[src_buf, pbuf_b[r]], writes=[pooled_b[m]])
            if ti == 0:
                tr.op(DVE, lambda e, srcA=srcA: e.tensor_tensor(tmp_t[:, 0:HALO], srcA(HALO, 2 * HALO), invc_t[:, g * 16:(g + 1) * 16], ALU.mult),
                      reads=[src_buf, small_b], writes=[tmp_b[0]])
                tr.op(DVE, lambda e: e.tensor_tensor(ar(A_POOLED, m, 0, HALO), tmp_t[:, 0:HALO], pbA(HALO, 2 * HALO), ALU.subtract),
                      reads=[tmp_b[0], pbuf_b[r], pooled_b[m]], writes=[pooled_b[m]])

        def mixer(ti):
            hb = hB()
            rmsnorm_to_xn(V_GM, stats_done=True)
            inherit(mixer_alias, hid_b)
            for which, dst_b, dstA in ((0, qT_b, qTA), (1, kT_b, kTA)):
                for s in range(2):
                    slot = acquire()
                    multi = None
                    if which == 0 and s == 0:
                        multi = [nb(), nb(), nb(), nb()]
                        pe_multi(multi, [0, 128, 256, 384], slot)
                    for hh in range(2):
                        head = 2 * s + hh
                        if multi is not None:
                            pa, pb = multi[2 * hh], multi[2 * hh + 1]
                        else:
                            pa, pb = nb(), nb()
                            pe_group(pa, xn_mms(pa, slot, hh * 256), reads=[ring_b[slot]])
                            pe_group(pb, xn_mms(pb, slot, hh * 256 + 128), reads=[ring_b[slot]])
                        rotary(pa, pb, dst_b, dstA, 2 * head)
                    release()
            for c in range(KC):
                tr.op(DVE, lambda e: e.tensor_tensor(arena_t[:, A_QD + c * T: A_QD + (c + 1) * T], qTA(c),
                                                     dq_t[:, (c // 2) * T:(c // 2 + 1) * T], ALU.mult),
                      reads=[qT_b[c], dq_b], writes=[qd_b[c]])
            for s in range(2):
                slot = acquire()
                for tc in range(4):
                    pv = nb()
                    pe_group(pv, [(bankA(pv), xnA(kc, tc * 128, (tc + 1) * 128), slab(slot, kc * 512, (kc + 1) * 512))
                                  for kc in range(KC)], reads=xn_b + [ring_b[slot]])
                    tr.op(ACT, lambda e, tc=tc, s=s, pv=pv: e.activation(vtokA(tc, s * 512, (s + 1) * 512), bankA(pv), AF.Copy),
                          reads=[bank_b[pv]], writes=[vtok_b[tc]])
                release()
            for s in range(2):
                slot = acquire()
                for mm in range(4):
                    m = 4 * s + mm
                    pg = nb()
                    pe_group(pg, xn_mms(pg, slot, mm * 128), reads=[ring_b[slot]])
                    tr.op(ACT, lambda e, m=m, pg=pg: e.activation(ar(A_SGR, m), bankA(pg), AF.Silu),
                          reads=[bank_b[pg]], writes=[sgr_b[m]])
                release()
            for s in range(2):
                slot = acquire()
                for mm in range(4):
                    m = 4 * s + mm
                    pp = nb()
                    pe_group(pp, xn_mms(pp, slot, mm * 128), reads=[ring_b[slot]])
                    pooling(ti, m, pp)
                release()
            slot = acquire()
            for g in range(4):
                for mm in range(2):
                    po = nb()
                    pe_group(po, [(bankA(po), slab(slot, g * 512 + kc * 256 + mm * 128, g * 512 + kc * 256 + mm * 128 + 128),
                                   ar(A_POOLED, 2 * g + kc)) for kc in range(2)],
                             reads=[pooled_b[2 * g], pooled_b[2 * g + 1], ring_b[slot]])
                    c = 2 * g + mm
                    tr.op(ACT, lambda e, c=c, po=po: e.activation(ar(A_POOLOUT, c), bankA(po), AF.Identity,
                                                                  scale=vecs_t[:, V_PS + c:V_PS + c + 1]),
                          reads=[bank_b[po], const_b], writes=[poolout_b[c]])
            release()
            for tc in range(4):
                pt = nb()

                def ftr(e, tc=tc, pt=pt):
                    ins = None
                    for f in range(KC):
                        ins = e.transpose(bankBF(pt, f * 128, (f + 1) * 128), kTA(f, tc * 128, (tc + 1) * 128), ident_t[:, :])
                    return ins
                tr.op(PE, ftr, reads=kT_b + [small_b], writes=[bank_b[pt]])
                for head in range(4):
                    tr.op(ACT, lambda e, tc=tc, pt=pt, head=head: e.activation(
                        ktokA(tc, head * 256, (head + 1) * 256), bankBF(pt, head * 256, (head + 1) * 256), AF.Identity,
                        scale=ksc_t[:, head * 4 + tc:head * 4 + tc + 1]),
                        reads=[bank_b[pt], small_b], writes=[ktok_b[tc]])
            inherit(retn_b, pooled_b)
            offs = [0, 512, 896, 1152]
            hstate = {}

            def scA(r, sbk, n):
                return arena_t[:, A_SC + r * SC_W + offs[sbk]: A_SC + r * SC_W + offs[sbk] + n]

            def qdA(c):
                return arena_t[:, A_QD + c * T: A_QD + (c + 1) * T]

            def stage_S(head):
                r = rot("sc", 2)
                for sbk in range(4):
                    n = (4 - sbk) * 128
                    psc = rot("scbank", 2)
                    pe_group(psc, [(bankA(psc, 0, n), kTA(2 * head + i, sbk * 128, (sbk + 1) * 128), qTA(2 * head + i, sbk * 128, T))
                                   for i in range(2)], reads=[kT_b[2 * head], kT_b[2 * head + 1], qT_b[2 * head], qT_b[2 * head + 1]])
                    tr.op(DVE, lambda e: e.tensor_tensor(scA(r, sbk, n), bankA(psc, 0, n), mask_t[:, head * 512: head * 512 + n], ALU.mult),
                          reads=[bank_b[psc], mask_b], writes=[sc_b[r]])
                hstate[head] = r

            def stage_O(head):
                r = hstate[head]
                pos = [2, 3] if head % 2 == 0 else [4, 5]
                pn = 6
                for ec in range(2):
                    po = pos[ec]
                    mms = []
                    for sbk in range(4):
                        n = (4 - sbk) * 128
                        mms.append((bankA(po, sbk * 128, T), vtokA(sbk, head * 256 + ec * 128, head * 256 + ec * 128 + 128),
                                    scA(r, sbk, n)))
                    for i in range(2):
                        mms.append((bankA(po), sbf_t[:, head * 512 + i * 256 + ec * 128: head * 512 + i * 256 + ec * 128 + 128],
                                    qdA(2 * head + i)))
                    pe_group(po, mms, reads=vtok_b + [sc_b[r], sbf_b[head], qd_b[2 * head], qd_b[2 * head + 1]])
                    rs = rot("sq", 2)
                    tr.op(ACT, lambda e: e.activation(sq_t[:, rs * T:(rs + 1) * T], bankA(po), AF.Square),
                          reads=[bank_b[po]], writes=[sq_b[rs]])
                    tr.op(PE, lambda e: e.matmul(bankA(pn), ones_t[:, :], sq_t[:, rs * T:(rs + 1) * T],
                                                 start=(ec == 0), stop=(ec == 1)),
                          reads=[sq_b[rs], small_b], writes=[bank_b[pn]])

            def stage_F(head):
                pos = [2, 3] if head % 2 == 0 else [4, 5]
                rstd_from(6, 1.0 / 256)
                for ec in range(2):
                    c = 2 * head + ec
                    rt = rot("tmp", 2)
                    tr.op(DVE, lambda e: e.tensor_tensor(tmp_t[:, rt * T:(rt + 1) * T], bankA(pos[ec]), rstd_t[:, :], ALU.mult),
                          reads=[bank_b[pos[ec]], rstd_b], writes=[tmp_b[rt]])
                    tr.op(DVE, lambda e: e.tensor_tensor(ar(A_RETN, c), tmp_t[:, rt * T:(rt + 1) * T], ar(A_SGR, c), ALU.mult),
                          reads=[tmp_b[rt], sgr_b[c]], writes=[retn_b[c]])
                pS = 7
                mms = []
                for i in range(2):
                    for tc in range(4):
                        mms.append((bankA(pS, i * 256, (i + 1) * 256), ktokA(tc, head * 256 + i * 128, head * 256 + i * 128 + 128),
                                    vtokA(tc, head * 256, head * 256 + 256)))

                def fst(e):
                    ins = None
                    for idx, (o, l, rr_) in enumerate(mms):
                        ins = e.matmul(o, l, rr_, start=(idx % 4 == 0), stop=(idx % 4 == 3))
                    return ins
                tr.op(PE, fst, reads=ktok_b + vtok_b, writes=[bank_b[pS]])
                cd = math.exp(LG[head] * T)
                tr.op(DVE, lambda e: e.scalar_tensor_tensor(s32_t[:, head * 512:(head + 1) * 512],
                                                            s32_t[:, head * 512:(head + 1) * 512], cd, bankA(pS),
                                                            ALU.mult, ALU.add),
                      reads=[bank_b[pS], s32_b[head]], writes=[s32_b[head]])
                tr.op(ACT, lambda e: e.activation(sbf_t[:, head * 512:(head + 1) * 512], s32_t[:, head * 512:(head + 1) * 512], AF.Copy),
                      reads=[s32_b[head]], writes=[sbf_b[head]])

            if os.environ.get("KDBG_SEQRET"):
                for hd in range(4):
                    stage_S(hd)
                    stage_O(hd)
                    stage_F(hd)
            else:
                stage_S(0)
                stage_S(1)
                stage_O(0)
                stage_S(2)
                stage_F(0)
                stage_O(1)
                stage_S(3)
                stage_F(1)
                stage_O(2)
                stage_F(2)
                stage_O(3)
                stage_F(3)
            inherit(merged_b, qT_b + kT_b + ktok_b)
            for m in range(KC):
                slot = acquire()
                pg0, pg1, pr, pp = nb(), nb(), nb(), nb()
                pe_group(pg0, xn_mms(pg0, slot, 0), reads=[ring_b[slot]])
                pe_group(pg1, xn_mms(pg1, slot, 128), reads=[ring_b[slot]])
                pe_group(pr, [(bankA(pr), slab(slot, kc * 512 + 256, kc * 512 + 384), ar(A_RETN, kc)) for kc in range(KC)],
                         reads=retn_b + [ring_b[slot]])
                pe_group(pp, [(bankA(pp), slab(slot, kc * 512 + 384, kc * 512 + 512), ar(A_POOLOUT, kc)) for kc in range(KC)],
                         reads=poolout_b + [ring_b[slot]])
                r0, r1 = rot("sg", 3), rot("sg", 3)
                tr.op(ACT, lambda e, r0=r0: e.activation(sg_t[:, r0 * T:(r0 + 1) * T], bankA(pg0), AF.Tanh, scale=0.5,
                                                         bias=gbh_t[:, m:m + 1]),
                      reads=[bank_b[pg0], small_b], writes=[sg_b[r0]])
                tr.op(ACT, lambda e, r1=r1: e.activation(sg_t[:, r1 * T:(r1 + 1) * T], bankA(pg1), AF.Tanh, scale=0.5,
                                                         bias=gbh_t[:, 8 + m:8 + m + 1]),
                      reads=[bank_b[pg1], small_b], writes=[sg_b[r1]])
                tr.op(DVE, lambda e, r0=r0: e.scalar_tensor_tensor(tmp_t[:, 0:T], sg_t[:, r0 * T:(r0 + 1) * T], 1.0, bankA(pr),
                                                                    ALU.add, ALU.mult),
                      reads=[sg_b[r0], bank_b[pr]], writes=[tmp_b[0]])
                tr.op(DVE, lambda e, r1=r1: e.scalar_tensor_tensor(tmp_t[:, T:2 * T], sg_t[:, r1 * T:(r1 + 1) * T], 1.0, bankA(pp),
                                                                    ALU.add, ALU.mult),
                      reads=[sg_b[r1], bank_b[pp]], writes=[tmp_b[1]])
                tr.op(DVE, lambda e: e.tensor_tensor(ar(A_MERGED, m), tmp_t[:, 0:T], tmp_t[:, T:2 * T], ALU.add),
                      reads=[tmp_b[0], tmp_b[1]], writes=[merged_b[m]])
                release()
            rr["sq"] = 0
            for s in range(2):
                slot = acquire()
                for mm in range(4):
                    m = 4 * s + mm
                    po = nb()
                    pe_group(po, [(bankA(po), slab(slot, kc * 512 + mm * 128, kc * 512 + mm * 128 + 128), ar(A_MERGED, kc), [merged_b[kc]])
                                  for kc in range(KC)], reads=[ring_b[slot]])
                    tr.op(DVE, lambda e, m=m, po=po: e.scalar_tensor_tensor(hA(m), bankA(po), 0.5, hA(m), ALU.mult, ALU.add),
                          reads=[bank_b[po], hb[m]], writes=[hb[m]])
                    norm_step(hb, hA, m)
                release()
            inherit(hid_b, mixer_alias)

        def sq_ones_task(q, kc):
            def t():
                if kc == 0:
                    rr["sq"] = 0
                norm_step(h_b2[q], hAq(q), kc)
            return t

        def prologue_tasks(ti):
            q = ti % 2
            t0 = ti * T

            def t_load():
                tr.dma(SP, lambda e: [e.dma_start(out=h_t[:, q * KC * T:(q + 1) * KC * T].rearrange("p (k t) -> p k t", t=T),
                                                  in_=xT.rearrange("(kc p) s -> p kc s", p=128)[:, :, t0:t0 + T])],
                       h_s2[q], writes=h_b2[q])
                gen_rope_tables(ti)
            return [t_load] + [sq_ones_task(q, kc) for kc in range(KC)]

        def prologue_finish(ti):
            q = ti % 2
            bg_flush()
            norm_finish(1.0 / D, NORMBANK)
            for kc in range(KC):
                tr.op(DVE, lambda e: e.scalar_tensor_tensor(xnA(kc), hAq(q)(kc), vecs_t[:, V_G1 + kc:V_G1 + kc + 1],
                                                            rstd_t[:, :], ALU.mult, ALU.mult),
                      reads=[h_b2[q][kc], rstd_b, const_b], writes=[xn_b[kc]])

        def final_tasks(ti):
            q = ti % 2
            t0 = ti * T
            tasks = [sq_ones_task(q, kc) for kc in range(KC)]
            tasks.append(lambda: norm_finish(1.0 / D, NORMBANK))

            def out_task(kc):
                def t():
                    r = rot("tmp", 2)
                    tr.op(DVE, lambda e: e.scalar_tensor_tensor(tmp_t[:, r * T:(r + 1) * T], hAq(q)(kc),
                                                                vecs_t[:, V_GF + kc:V_GF + kc + 1], rstd_t[:, :],
                                                                ALU.mult, ALU.mult),
                          reads=[h_b2[q][kc], rstd_b, const_b], writes=[tmp_b[r]])
                    tr.dma(SP, lambda e: [e.dma_start(out=outT[kc * 128:(kc + 1) * 128, t0:t0 + T],
                                                      in_=tmp_t[:, r * T:(r + 1) * T])],
                           ob_s[r], reads=[tmp_b[r]])
                return t
            return tasks + [out_task(kc) for kc in range(KC)]

        bg.extend(prologue_tasks(0))
        prologue_finish(0)
        for ti in range(NT):
            hsel[0] = ti % 2
            ffn(post_res=lambda m: norm_step(hB(), hA, m))
            bg_flush()
            mixer(ti)
            rmsnorm_to_xn(V_G2, stats_done=True)
            if ti + 1 < NT:
                bg.extend(prologue_tasks(ti + 1))
                ffn(between=lambda: prologue_finish(ti + 1), paced=True)
            else:
                ffn(paced=True)
            bg_flush()
            bg.extend(final_tasks(ti))
        bg_flush()
        for r in range(2):
            SP.h.wait_ge(ob_s[r].sem, ob_s[r].count)
    return nc


_CACHE = {}


def _consts():
    c = np.zeros((128, NCONST), np.float32)
    half = 128
    invf = (10000.0 ** (-np.arange(half, dtype=np.float32) / half)).astype(np.float32)
    c[:, C_INVF] = invf
    c[:, C_TLOC:C_TLOC + T] = np.arange(T, dtype=np.float32)[None, :]
    c[:, C_ID:C_ID + 128] = np.eye(128, dtype=np.float32)
    cs = np.zeros((128, NSET), np.float32)
    p = np.arange(128)
    for delta in range(4):
        cc = np.arange(128)[None, :] + 128 * delta
        ss = p[:, None]
        cs[:, S_DIST + delta * 128:S_DIST + (delta + 1) * 128] = np.abs(cc - ss).astype(np.float32)
    cc = np.arange(128)[None, :]
    ss = p[:, None]
    valid = ((ss // 64) <= (cc // 64)).astype(np.float32)
    cs[:, S_VALID:S_VALID + 128] = valid
    for tc in range(4):
        cs[:, S_KREV + tc] = (T - 1 - (tc * 128 + p)).astype(np.float32)
    for g, w in enumerate((2, 4, 8, 16)):
        cs[:, S_CNT + g * 16:S_CNT + (g + 1) * 16] = np.minimum(np.arange(16) + 1.0, float(w))[None, :]
    return c, cs


def _pvec(v):
    return np.ascontiguousarray(np.asarray(v, np.float32).reshape(-1, 128).T)


def kernel(x, norm_ffn1, ffn1_w_in, ffn1_w_out, norm_mix, w_in, gate_bias, pool_w, pool_scale,
           w_ret_up, w_pool_up, w_out, norm_ffn2, ffn2_w_in, ffn2_w_out, norm_final):
    x = np.asarray(x, np.float32)
    B = x.shape[0]
    if "nc" not in _CACHE:
        _CACHE["nc"] = build_program()
    nc = _CACHE["nc"]
    vecs = np.zeros((128, NVEC), np.float32)
    vecs[:, V_G1:V_G1 + 8] = _pvec(norm_ffn1[0])
    vecs[:, V_GM:V_GM + 8] = _pvec(norm_mix[0])
    vecs[:, V_G2:V_G2 + 8] = _pvec(norm_ffn2[0])
    vecs[:, V_GF:V_GF + 8] = _pvec(norm_final)
    vecs[:, V_GB0:V_GB0 + 8] = _pvec(np.asarray(gate_bias)[0, 0])
    vecs[:, V_GB1:V_GB1 + 8] = _pvec(np.asarray(gate_bias)[0, 1])
    vecs[:, V_PS:V_PS + 8] = _pvec(np.asarray(pool_scale)[0])
    consts, cset = _consts()
    f = lambda a: np.ascontiguousarray(np.asarray(a, np.float32))
    shared = {
        "w1i": f(ffn1_w_in[0]), "w1o": f(ffn1_w_out[0]), "wi": f(w_in[0]), "pw": f(pool_w[0]),
        "wru": f(w_ret_up[0]), "wpu": f(w_pool_up[0]), "wo": f(w_out[0]),
        "w2i": f(ffn2_w_in[0]), "w2o": f(ffn2_w_out[0]), "consts": consts, "cset": cset, "vecs": vecs,
    }
    in_maps = []
    for b in range(B):
        m = dict(shared)
        m["xT"] = np.ascontiguousarray(x[b].T)
        in_maps.append(m)
    res = run_bass_kernel_spmd(nc, in_maps, core_ids=list(range(B)))
    out = np.empty((B, S, D), np.float32)
    for b in range(B):
        out[b] = res.results[b]["outT"].T
    return out
```
